# Optimizing a Trainium2 kernel written in Bass

```python
import jax, jax.numpy as jnp
from jax import lax
import numpy as np

D_MODEL = 1024
BATCH = 8
SEQ = 4096
DEPTH = 4

N_A_LAYERS = DEPTH // 2
N_B_LAYERS = DEPTH - N_A_LAYERS
PLE_DIM = 256
D_FF = 4 * D_MODEL
NORM_EPS = 1e-6

A_HEADS = 8
A_KEY_DIM = 128
A_VAL_DIM = 128
A_CONV = 4
A_CHUNK = 64
A_QK = A_HEADS * A_KEY_DIM
A_V = A_HEADS * A_VAL_DIM
A_CONV_CH = 2 * A_QK + A_V
A_IN_COLS = 2 * A_QK + 2 * A_V + 2 * A_HEADS

B_HEAD_DIM = 128
B_GROUPS = ((128, 1), (512, 4), (2048, 16))
B_N_GROUPS = len(B_GROUPS)
B_HEADS_PER_GROUP = 8
B_KV_PER_GROUP = 2
B_Q_COLS = B_N_GROUPS * B_HEADS_PER_GROUP * B_HEAD_DIM
B_KV_COLS = 2 * B_N_GROUPS * B_KV_PER_GROUP * B_HEAD_DIM
B_OUT = B_HEADS_PER_GROUP * B_HEAD_DIM

ROPE_THETA = 500000.0
ROPE_DIM = B_HEAD_DIM // 4

kernel_name = "yoco_gdn_dilated_swa_hybrid"

F32 = jnp.float32


def rms_norm(x, w):
    xf = x.astype(F32)
    y = xf * lax.rsqrt(jnp.mean(xf * xf, axis=-1, keepdims=True) + NORM_EPS)
    return (y * w.astype(F32)).astype(x.dtype)


def l2_norm(x):
    return x * lax.rsqrt(jnp.sum(x * x, axis=-1, keepdims=True) + NORM_EPS)


def rope_tables(seq):
    inv = jnp.power(jnp.float32(ROPE_THETA), -jnp.arange(0, ROPE_DIM, 2, dtype=F32) / ROPE_DIM)
    ang = jnp.arange(seq, dtype=F32)[:, None] * inv[None, :]
    return jnp.cos(ang), jnp.sin(ang)


def apply_partial_rope(x, cos, sin):
    half = ROPE_DIM // 2
    shape = (1, x.shape[1]) + (1,) * (x.ndim - 3) + (half,)
    c = cos.reshape(shape)
    s = sin.reshape(shape)
    xf = x.astype(F32)
    x1 = xf[..., :half]
    x2 = xf[..., half:ROPE_DIM]
    return jnp.concatenate([x1 * c - x2 * s, x2 * c + x1 * s, xf[..., ROPE_DIM:]], axis=-1).astype(x.dtype)


def causal_depthwise_conv(x, w):
    c = x.shape[-1]
    return lax.conv_general_dilated(x, w[:, None, :].astype(x.dtype), window_strides=(1,),
                                    padding=[(w.shape[0] - 1, 0)],
                                    dimension_numbers=('NWC', 'WIO', 'NWC'),
                                    feature_group_count=c)


def chunk_gated_delta_rule(q, k, v, g, beta):
    bsz, seq, heads, dk = q.shape
    dv = v.shape[-1]
    c = A_CHUNK
    n = seq // c

    def chunks(t):
        return jnp.moveaxis(t.reshape((bsz, n, c, heads) + t.shape[3:]), 3, 1)

    q = chunks(q) * (dk ** -0.5)
    k = chunks(k)
    v = chunks(v)
    g = chunks(g)
    beta = chunks(beta)
    gc = jnp.cumsum(g, axis=-1)
    idx = jnp.arange(c)
    incl = idx[:, None] >= idx[None, :]
    strict = idx[:, None] > idx[None, :]
    decay = jnp.exp(jnp.where(incl, gc[..., :, None] - gc[..., None, :], -jnp.inf))
    kb = k * beta[..., None]
    lower = jnp.where(strict, jnp.einsum('bhncd,bhnjd->bhncj', kb, k) * decay, 0.0)
    eye = jnp.eye(c, dtype=F32)
    tmat = lax.linalg.triangular_solve(lower + eye, jnp.broadcast_to(eye, lower.shape),
                                       left_side=True, lower=True, unit_diagonal=True)
    u = tmat @ (v * beta[..., None])
    w = tmat @ (kb * jnp.exp(gc)[..., None])
    attn = jnp.where(incl, jnp.einsum('bhncd,bhnjd->bhncj', q, k) * decay, 0.0)
    q_dec = q * jnp.exp(gc)[..., None]
    k_dec = k * jnp.exp(gc[..., -1:] - gc)[..., None]
    chunk_dec = jnp.exp(gc[..., -1])

    def step(state, xs):
        u_n, w_n, attn_n, qd_n, kd_n, cd_n = xs
        v_new = u_n - jnp.einsum('bhcd,bhde->bhce', w_n, state)
        o_n = jnp.einsum('bhcd,bhde->bhce', qd_n, state) + jnp.einsum('bhcj,bhje->bhce', attn_n, v_new)
        state = state * cd_n[..., None, None] + jnp.einsum('bhcd,bhce->bhde', kd_n, v_new)
        return state, o_n

    xs = (jnp.moveaxis(u, 2, 0), jnp.moveaxis(w, 2, 0), jnp.moveaxis(attn, 2, 0),
          jnp.moveaxis(q_dec, 2, 0), jnp.moveaxis(k_dec, 2, 0), jnp.moveaxis(chunk_dec, 2, 0))
    state0 = jnp.zeros((bsz, heads, dk, dv), F32)
    _, o = lax.scan(step, state0, xs)
    return jnp.transpose(o, (1, 0, 3, 2, 4)).reshape(bsz, seq, heads, dv)


def gated_deltanet(h, w_in, conv_w, a_log, dt_bias, out_norm, w_out):
    bsz, seq, _ = h.shape
    proj = h @ w_in
    qkv = jax.nn.silu(causal_depthwise_conv(proj[..., :A_CONV_CH], conv_w)).astype(F32)
    q = l2_norm(qkv[..., :A_QK].reshape(bsz, seq, A_HEADS, A_KEY_DIM))
    k = l2_norm(qkv[..., A_QK:2 * A_QK].reshape(bsz, seq, A_HEADS, A_KEY_DIM))
    v = qkv[..., 2 * A_QK:].reshape(bsz, seq, A_HEADS, A_VAL_DIM)
    z = proj[..., A_CONV_CH:A_CONV_CH + A_V].astype(F32).reshape(bsz, seq, A_HEADS, A_VAL_DIM)
    b = proj[..., A_CONV_CH + A_V:A_CONV_CH + A_V + A_HEADS].astype(F32)
    a = proj[..., A_CONV_CH + A_V + A_HEADS:].astype(F32)
    beta = jax.nn.sigmoid(b)
    g = -jnp.exp(a_log.astype(F32)) * jax.nn.softplus(a + dt_bias.astype(F32))
    o = chunk_gated_delta_rule(q, k, v, g, beta)
    o = o * lax.rsqrt(jnp.mean(o * o, axis=-1, keepdims=True) + NORM_EPS) * out_norm.astype(F32) * jax.nn.silu(z)
    return o.reshape(bsz, seq, A_V).astype(h.dtype) @ w_out


def dilated_group_attention(q, k, v, window, dilation):
    bsz, seq, hq, dh = q.shape
    hkv = k.shape[2]
    rep = hq // hkv
    blk = window // dilation
    length = seq // dilation
    nb = -(-length // blk)
    lp = nb * blk
    z = bsz * dilation

    def to_blocks(t):
        hh = t.shape[2]
        t = t.reshape(bsz, length, dilation, hh, dh).transpose(0, 2, 1, 3, 4).reshape(z, length, hh, dh)
        t = jnp.pad(t, ((0, 0), (0, lp - length), (0, 0), (0, 0)))
        return t.reshape(z, nb, blk, hh, dh)

    def with_prev(t):
        prev = jnp.pad(t, ((0, 0), (1, 0), (0, 0), (0, 0), (0, 0)))[:, :-1]
        return jnp.concatenate([prev, t], axis=2)

    qb = to_blocks(q).reshape(z, nb, blk, hkv, rep, dh)
    kk = with_prev(to_blocks(k))
    vv = with_prev(to_blocks(v))
    s = jnp.einsum('znqgrd,znkgd->zngrqk', qb, kk, preferred_element_type=F32) * (dh ** -0.5)
    qi = jnp.arange(blk)[:, None]
    kj = jnp.arange(2 * blk)[None, :]
    dist = qi + blk - kj
    band = (dist >= 0) & (dist <= blk)
    mask = band[None] & ((jnp.arange(nb)[:, None, None] > 0) | (kj >= blk)[None])
    s = jnp.where(mask[None, :, None, None], s, -jnp.inf)
    m = jnp.max(s, axis=-1, keepdims=True)
    e = jnp.exp(s - m)
    den = jnp.sum(e, axis=-1)
    o = jnp.einsum('zngrqk,znkgd->znqgrd', e, vv.astype(F32)) / jnp.transpose(den, (0, 1, 4, 2, 3))[..., None]
    lse = jnp.transpose(m[..., 0] + jnp.log(den), (0, 1, 4, 2, 3))
    o = o.reshape(z, lp, hq, dh)[:, :length]
    lse = lse.reshape(z, lp, hq)[:, :length]
    o = o.reshape(bsz, dilation, length, hq, dh).transpose(0, 2, 1, 3, 4).reshape(bsz, seq, hq, dh)
    lse = lse.reshape(bsz, dilation, length, hq).transpose(0, 2, 1, 3).reshape(bsz, seq, hq)
    return o, lse


def dilated_mixture_attention(h, k_shared, v_shared, w_q, w_o, cos, sin):
    bsz, seq, _ = h.shape
    q = (h @ w_q).reshape(bsz, seq, B_N_GROUPS, B_HEADS_PER_GROUP, B_HEAD_DIM)
    q = apply_partial_rope(q, cos, sin)
    outs = []
    lses = []
    for gi, (window, dilation) in enumerate(B_GROUPS):
        o_g, lse_g = dilated_group_attention(q[:, :, gi], k_shared[:, :, gi], v_shared[:, :, gi], window, dilation)
        outs.append(o_g)
        lses.append(lse_g)
    wts = jax.nn.softmax(jnp.stack(lses, axis=0), axis=0)
    o = jnp.sum(wts[..., None] * jnp.stack(outs, axis=0), axis=0)
    return o.reshape(bsz, seq, B_OUT).astype(h.dtype) @ w_o


def squared_relu_mlp(h, w_up, w_down):
    return jnp.square(jax.nn.relu(h @ w_up)) @ w_down


def setup_inputs(seed: int = 0) -> dict:
    key = jax.random.key(seed)
    ks = jax.random.split(key, 20)

    def dense(k, shape, fan_in):
        return jax.random.normal(k, shape, F32) * (fan_in ** -0.5)

    def gain(k, shape):
        return 1.0 + 0.02 * jax.random.normal(k, shape, F32)

    return {
        'x': jax.random.normal(ks[0], (BATCH, SEQ, D_MODEL), F32),
        'p': jax.random.normal(ks[1], (DEPTH, BATCH, SEQ, PLE_DIM), F32),
        'attn_norm': gain(ks[2], (DEPTH, D_MODEL)),
        'mlp_norm': gain(ks[3], (DEPTH, D_MODEL)),
        'a_w_in': dense(ks[4], (N_A_LAYERS, D_MODEL, A_IN_COLS), D_MODEL),
        'a_conv_w': dense(ks[5], (N_A_LAYERS, A_CONV, A_CONV_CH), A_CONV),
        'a_log': jnp.log(jax.random.uniform(ks[6], (N_A_LAYERS, A_HEADS), F32, 1.0, 16.0)),
        'a_dt_bias': 0.1 * jax.random.normal(ks[7], (N_A_LAYERS, A_HEADS), F32),
        'a_out_norm': gain(ks[8], (N_A_LAYERS, A_VAL_DIM)),
        'a_w_out': dense(ks[9], (N_A_LAYERS, A_V, D_MODEL), A_V),
        'kv_norm': gain(ks[10], (D_MODEL,)),
        'b_w_kv': dense(ks[11], (D_MODEL, B_KV_COLS), D_MODEL),
        'b_w_q': dense(ks[12], (N_B_LAYERS, D_MODEL, B_Q_COLS), D_MODEL),
        'b_w_o': dense(ks[13], (N_B_LAYERS, B_OUT, D_MODEL), B_OUT),
        'mlp_w_up': dense(ks[14], (DEPTH, D_MODEL, D_FF), D_MODEL),
        'mlp_w_down': dense(ks[15], (DEPTH, D_FF, D_MODEL), D_FF),
        'ple_w_proj': dense(ks[16], (DEPTH, PLE_DIM, D_MODEL), PLE_DIM),
        'ple_w_gate': dense(ks[17], (DEPTH, D_MODEL, D_MODEL), D_MODEL),
        'final_norm': gain(ks[18], (D_MODEL,)),
    }


def reference(x, p, attn_norm, mlp_norm, a_w_in, a_conv_w, a_log, a_dt_bias, a_out_norm, a_w_out,
              kv_norm, b_w_kv, b_w_q, b_w_o, mlp_w_up, mlp_w_down, ple_w_proj, ple_w_gate, final_norm):
    bsz, seq, _ = x.shape
    cos, sin = rope_tables(seq)
    k_shared = None
    v_shared = None
    for i in range(DEPTH):
        if i < N_A_LAYERS:
            h = rms_norm(x, attn_norm[i])
            x = x + gated_deltanet(h, a_w_in[i], a_conv_w[i], a_log[i], a_dt_bias[i], a_out_norm[i], a_w_out[i])
        else:
            if i == N_A_LAYERS:
                kv = (rms_norm(x, kv_norm) @ b_w_kv).reshape(bsz, seq, 2, B_N_GROUPS, B_KV_PER_GROUP, B_HEAD_DIM)
                k_shared = apply_partial_rope(kv[:, :, 0], cos, sin)
                v_shared = kv[:, :, 1]
            j = i - N_A_LAYERS
            h = rms_norm(x, attn_norm[i])
            x = x + dilated_mixture_attention(h, k_shared, v_shared, b_w_q[j], b_w_o[j], cos, sin)
        h = rms_norm(x, mlp_norm[i])
        x = x + squared_relu_mlp(h, mlp_w_up[i], mlp_w_down[i])
        x = x + jax.nn.sigmoid(x @ ple_w_gate[i]) * (p[i].astype(x.dtype) @ ple_w_proj[i])
    return rms_norm(x, final_norm)
```

```python
import numpy as np
import ml_dtypes
import concourse.bass as bass
import concourse.mybir as mybir
from concourse.bass_utils import run_bass_kernel_spmd

F32 = mybir.dt.float32
BF16 = mybir.dt.bfloat16
U8 = mybir.dt.uint8
AF = mybir.ActivationFunctionType
ALU = mybir.AluOpType

D = 1024
KC = 8
DFF = 4096
PLE = 256
EPS = 1e-6
T = 1024
HALF = 512
NEG = -30000.0


class Tok:
    __slots__ = ("w", "r", "name", "excl")

    def __init__(self, name="", excl=False):
        self.w = None
        self.r = {}
        self.name = name
        self.excl = excl


class Sched:
    ENGS = ["pe", "act", "dve", "pool", "sp"]

    def __init__(self, nc):
        self.nc = nc
        self.q = {e: [] for e in self.ENGS}
        self.cnt = {e: 0 for e in self.ENGS}
        self.waited = {e: {} for e in self.ENGS}
        self.dmacnt = {}

    def _deps(self, eng, reads, writes):
        deps = {}

        def add(k, v):
            if v > deps.get(k, 0):
                deps[k] = v
        for t in reads:
            if t.w is not None:
                add(*t.w)
        for t in writes:
            if t.w is not None:
                add(*t.w)
            for k, v in t.r.items():
                add(k, v)
        waits = []
        for k, v in deps.items():
            if k == eng and eng == "pe":
                continue
            if self.waited[eng].get(k, 0) < v:
                self.waited[eng][k] = v
                waits.append((k, v))
        return waits

    def op(self, eng, fn, reads=(), writes=()):
        ex = [t for t in reads if t.excl]
        if ex:
            reads = [t for t in reads if not t.excl]
            writes = list(writes) + ex
        waits = self._deps(eng, reads, writes)
        self.cnt[eng] += 1
        n = self.cnt[eng]
        self.q[eng].append((waits, fn, (eng, 1)))
        for t in reads:
            t.r[eng] = n
        for t in writes:
            t.w = (eng, n)
            t.r = {}

    def dma(self, queue, semkey, fn, reads=(), writes=()):
        waits = self._deps(queue, reads, writes)
        self.dmacnt[semkey] = self.dmacnt.get(semkey, 0) + 16
        n = self.dmacnt[semkey]
        self.q[queue].append((waits, fn, (semkey, 16)))
        for t in reads:
            t.r[semkey] = n
        for t in writes:
            t.w = (semkey, n)
            t.r = {}

    def wait_all(self, eng, toks):
        waits = self._deps(eng, toks, ())
        self.q[eng].append((waits, None, None))

    def emit(self):
        nc = self.nc
        keys = list(self.ENGS) + list(self.dmacnt.keys())
        waited = {k: set() for k in self.ENGS}
        for eng in self.ENGS:
            for waits, fn, inc in self.q[eng]:
                for k, v in waits:
                    if k in waited:
                        waited[k].add(v)
        rank = {k: {v: i + 1 for i, v in enumerate(sorted(waited[k]))} for k in self.ENGS}
        import contextlib
        with contextlib.ExitStack() as es:
            sems = {k: es.enter_context(nc.semaphore("s_" + k)) for k in keys}
            block = es.enter_context(nc.Block())

            def replay(eng, e):
                idx = 0
                for waits, fn, inc in self.q[eng]:
                    for k, v in waits:
                        e.wait_ge(sems[k], rank[k][v] if k in rank else v)
                    if fn is not None:
                        ins = fn(e)
                        if inc[0] == eng:
                            idx += 1
                            if idx in rank[eng]:
                                ins.then_inc(sems[eng], 1)
                        else:
                            ins.then_inc(sems[inc[0]], inc[1])

            @block.tensor
            def _(e):
                replay("pe", e)

            @block.scalar
            def _(e):
                replay("act", e)

            @block.vector
            def _(e):
                replay("dve", e)

            @block.gpsimd
            def _(e):
                replay("pool", e)

            @block.sync
            def _(e):
                replay("sp", e)


class Arena:
    def __init__(self, nc, name, nbytes):
        self.t = nc.alloc_sbuf_tensor(name, [128, nbytes], U8)
        self.n = nbytes
        self.off = 0

    def take(self, dtype, shape, at=None):
        esz = 2 if dtype == BF16 else 4
        n = esz
        for s in shape:
            n *= s
        if at is None:
            at = self.off
            self.off += (n + 31) // 32 * 32
            assert self.off <= self.n, ("arena overflow", self.off, self.n)
        assert at + n <= self.n
        ap = self.t[:, at:at + n].bitcast(dtype)
        if len(shape) == 2:
            ap = ap.rearrange("p (a b) -> p a b", a=shape[0])
        elif len(shape) == 3:
            ap = ap.rearrange("p (a b c) -> p a b c", a=shape[0], b=shape[1])
        return ap


class Gen:
    def __init__(self, S, n_layers=4, n_a=2, mixers=True, stage=99):
        self.S = S
        self.stage = stage
        self.NT = S // T
        self.n_layers = n_layers
        self.n_a = n_a
        self.mixers = mixers
        self.nc = bass.Bass("TRN2", target_bir_lowering=False)
        self.sc = Sched(self.nc)
        self.consts_np = {}

    def dram_in(self, name, shape, dtype=F32):
        return self.nc.dram_tensor(name, list(shape), dtype, kind="ExternalInput").ap()

    def declare(self):
        nc, S = self.nc, self.S
        L = self.n_layers
        self.xT_in = self.dram_in("xT", [D, S])
        self.pT_in = self.dram_in("pT", [L, PLE, S])
        self.attn_norm = self.dram_in("attn_norm", [L, D])
        self.mlp_norm = self.dram_in("mlp_norm", [L, D])
        self.final_norm = self.dram_in("final_norm", [D])
        self.w_up = self.dram_in("mlp_w_up", [L, D, DFF])
        self.w_down = self.dram_in("mlp_w_down", [L, DFF, D])
        self.w_pproj = self.dram_in("ple_w_proj", [L, PLE, D])
        self.w_pgate = self.dram_in("ple_w_gate", [L, D, D])
        na = max(1, self.n_a)
        self.a_w_in = self.dram_in("a_w_in", [na, D, 4112])
        self.a_w_out = self.dram_in("a_w_out", [na, D, D])
        self.d_convw = self.dram_in("convw", [na, 128, 4, 24])
        self.d_alog = self.dram_in("alog_rep", [na, 128, 64])
        self.d_dtb = self.dram_in("dtb_rep", [na, 128, 64])
        self.d_onw = self.dram_in("onw", [na, 128, 1])
        self.t_dbg = Tok("dbg")
        nb = max(1, self.n_layers - self.n_a)
        self.b_w_kv = self.dram_in("b_w_kv", [D, 1536])
        self.b_w_q = self.dram_in("b_w_q", [nb, D, 3072])
        self.b_w_o = self.dram_in("b_w_o", [nb, D, D])
        self.kv_norm = self.dram_in("kv_norm", [D])
        self.K_dram = nc.dram_tensor("K_scr", [6, 128, S], BF16, kind="Internal").ap()
        self.V_dram = nc.dram_tensor("V_scr", [3, S, 256], BF16, kind="Internal").ap()
        self.t_kdram2 = [Tok("kdram0"), Tok("kdram1")]
        self.t_vdram2 = [Tok("vdram%d" % i) for i in range(4)]
        self.outT = nc.dram_tensor("outT", [D, S], F32, kind="ExternalOutput").ap()
        self.x_scr = nc.dram_tensor("x_scr", [D, S], F32, kind="Internal").ap()

    def alloc(self):
        nc = self.nc
        self.ar = Arena(nc, "arena", 207 * 1024)
        ar = self.ar
        self.ident = ar.take(BF16, [128])
        self.ones = ar.take(BF16, [128])
        self.normw = ar.take(F32, [2 * self.n_layers + 2, KC])
        self.tri_bf = ar.take(BF16, [128])
        self.maskS_bf = ar.take(BF16, [128])
        self.C1_bf = ar.take(BF16, [128])
        self.C2_bf = ar.take(BF16, [128])
        self.scalb = ar.take(BF16, [6, 64])
        self.convw = ar.take(F32, [4, 24])
        self.alog = ar.take(F32, [64])
        self.dtb = ar.take(F32, [64])
        self.nexpA = ar.take(F32, [64])
        self.onw = ar.take(F32, [1])
        self.halo = ar.take(BF16, [24, 3])
        self.st_off = ar.off
        self.Sst = ar.take(F32, [8, 128])
        self.Sbf = ar.take(BF16, [8, 128])
        self.scal = ar.take(F32, [17, 64])
        self.protT = ar.take(BF16, [128], at=self.st_off)
        self.maskbuf = ar.take(BF16, [640], at=self.st_off + 256)
        self.t_scal = Tok("scal")
        self.t_scalb = Tok("scalb")
        self.t_S = Tok("S")
        self.t_Sbf = Tok("Sbf")
        self.t_Sg = [Tok("S0"), Tok("S1")]
        self.t_Sbfg = [Tok("Sbf0"), Tok("Sbf1")]
        self.t_halo = Tok("halo")
        self.t_lconst = Tok("lconst")
        self.xT = ar.take(F32, [KC, T])
        self.hT = ar.take(BF16, [KC, T])
        self.rstd = ar.take(F32, [T])
        self.tmpf = [ar.take(F32, [HALF]) for _ in range(2)]
        self.NSLAB = 3
        self.slabs = [ar.take(BF16, [4096]) for _ in range(self.NSLAB)]
        self.slab_tok = [Tok("slab%d" % i) for i in range(self.NSLAB)]
        self.slab_i = 0
        self.big_off = ar.off
        self.big_n = ar.n - ar.off - 8192
        self.pT = ar.take(BF16, [2, T], at=ar.n - 8192)
        self.pwp = ar.take(BF16, [2, D], at=ar.n - 4096)
        self.ps = [nc.alloc_psum_tensor("ps%d" % i, [128, 2048], U8) for i in range(8)]
        self.ps_tok = [Tok("ps%d" % i, excl=True) for i in range(8)]
        self.t_xk = [Tok("x%d" % k) for k in range(KC)]
        self.t_h = Tok("h")
        self.t_rstd = Tok("rstd")
        self.t_tmpf = [Tok("tmpf%d" % i) for i in range(2)]
        self.t_const = Tok("const")
        self.t_xscr = [[Tok("xscr%d_%d" % (i, k)) for k in range(KC)] for i in range(self.NT)]
        self.t_out = Tok("out")

    def psf(self, b):
        return self.ps[b][:, :].bitcast(F32)

    def psb(self, b):
        return self.ps[b][:, :].bitcast(BF16)

    def big(self, dtype, shape, at):
        return self.ar.take(dtype, shape, at=self.big_off + at)

    def add_const(self, name, arr):
        arr = np.ascontiguousarray(arr)
        self.consts_np[name] = arr
        dt = BF16 if arr.dtype == ml_dtypes.bfloat16 else F32
        return self.dram_in(name, arr.shape, dt)

    def setup_consts(self):
        sc = self.sc
        ident = np.eye(128, dtype=np.float32).astype(ml_dtypes.bfloat16)
        ones = np.ones((128, 128), dtype=np.float32).astype(ml_dtypes.bfloat16)
        d_ident = self.add_const("c_ident", ident)
        d_ones = self.add_const("c_ones", ones)
        sc.dma("sp", "cst", lambda e: e.dma_start(out=self.ident, in_=d_ident), writes=[self.t_const])
        sc.dma("sp", "cst", lambda e: e.dma_start(out=self.ones, in_=d_ones), writes=[self.t_const])
        ar_ = np.arange(128)
        BIGC = 100.0
        fcs = {
            "c_tri": (ar_[:, None] <= ar_[None, :]).astype(np.float32).astype(ml_dtypes.bfloat16),
            "c_maskS": (ar_[:, None] > ar_[None, :]).astype(np.float32).astype(ml_dtypes.bfloat16),
            "c_C1": (BIGC * (ar_[:, None] > ar_[None, :])).astype(np.float32).astype(ml_dtypes.bfloat16),
            "c_C2": (BIGC * (ar_[:, None] <= ar_[None, :])).astype(np.float32).astype(ml_dtypes.bfloat16),
        }
        for nm, dst in [("c_tri", self.tri_bf), ("c_maskS", self.maskS_bf), ("c_C1", self.C1_bf), ("c_C2", self.C2_bf)]:
            d_ = self.add_const(nm, fcs[nm])
            sc.dma("sp", "cst", lambda e, d_=d_, dst=dst: e.dma_start(out=dst, in_=d_), writes=[self.t_const])
        L = self.n_layers
        for i in range(L):
            sc.dma("sp", "cst", lambda e, i=i: e.dma_start(
                out=self.normw[:, i, :], in_=self.attn_norm[i].rearrange("(kc p) -> p kc", p=128),
                allow_slow_non_contiguous=True), writes=[self.t_const])
            sc.dma("sp", "cst", lambda e, i=i: e.dma_start(
                out=self.normw[:, L + i, :], in_=self.mlp_norm[i].rearrange("(kc p) -> p kc", p=128),
                allow_slow_non_contiguous=True), writes=[self.t_const])
        sc.dma("sp", "cst", lambda e: e.dma_start(
            out=self.normw[:, 2 * L, :], in_=self.final_norm.rearrange("(kc p) -> p kc", p=128),
            allow_slow_non_contiguous=True), writes=[self.t_const])
        sc.dma("sp", "cst", lambda e: e.dma_start(
            out=self.normw[:, 2 * L + 1, :], in_=self.kv_norm.rearrange("(kc p) -> p kc", p=128),
            allow_slow_non_contiguous=True), writes=[self.t_const])
        sc.op("dve", lambda e: e.tensor_scalar(self.normw, self.normw, float(np.sqrt(D)), None, ALU.mult),
              reads=[self.t_const], writes=[self.t_const])

    def load_slab(self, src_ap, kc, nw):
        i = self.slab_i % self.NSLAB
        self.slab_i += 1
        assert kc * nw <= 4096
        dst = self.slabs[i][:, 0:kc * nw].rearrange("p (a b) -> p a b", a=kc)
        tok = self.slab_tok[i]
        self.sc.dma("pool", "slab%d" % i, lambda e: e.dma_start(out=dst, in_=src_ap), writes=[tok])
        return dst, tok

    def linear_fm(self, *a, **kw):
        for _ in self.linear_fm_g(*a, **kw):
            pass

    def linear_fm_g(self, w2d, K, n0, n1, rhs, rhs_toks, evac, nw=None, banks=(0, 1), ntok=T, halves=None):
        sc = self.sc
        kc = K // 128
        if nw is None:
            nw = 4096 // kc
        if halves is None:
            halves = range(ntok // HALF)
        wv = w2d.rearrange("(kc p) n -> p kc n", p=128)
        bi = 0
        for s0 in range(n0, n1, nw):
            s1 = min(s0 + nw, n1)
            slab, stok = self.load_slab(wv[:, :, s0:s1], kc, s1 - s0)
            for c0 in range(0, s1 - s0, 128):
                cw = min(128, s1 - s0 - c0)
                for h in halves:
                    b = banks[bi % len(banks)]
                    bi += 1
                    pt = self.ps_tok[b]
                    out = self.psf(b)[0:cw, :]
                    for k in range(kc):
                        sc.op("pe", lambda e, out=out, slab=slab, k=k, c0=c0, cw=cw, h=h: e.matmul(
                            out, slab[:, k, c0:c0 + cw], rhs[:, k, h * HALF:(h + 1) * HALF],
                            start=(k == 0), stop=(k == kc - 1)),
                            reads=[stok] + list(rhs_toks), writes=[pt])
                        if k % 8 == 7 and k != kc - 1:
                            yield
                    evac((s0 + c0 - n0) // 128, h, out, pt)
                    yield

    def rmsnorm(self, widx, out_ap, out_tok, sq_ap, sq_tok, halves=(0, 1), banks=(2, 3)):
        sc = self.sc
        for h in halves:
            sl = slice(h * HALF, (h + 1) * HALF)
            for k in range(KC):
                sc.op("act", lambda e, k=k, sl=sl: e.activation(out=sq_ap[:, k, sl], in_=self.xT[:, k, sl], func=AF.Square),
                      reads=[self.t_xk[k]], writes=[sq_tok])
        for i_, h in enumerate(halves):
            b = banks[i_ % len(banks)]
            pt = self.ps_tok[b]
            out = self.psf(b)
            for k in range(KC):
                sc.op("pe", lambda e, out=out, k=k, h=h: e.matmul(
                    out, self.ones, sq_ap[:, k, h * HALF:(h + 1) * HALF], start=(k == 0), stop=(k == KC - 1)),
                    reads=[sq_tok, self.t_const], writes=[pt])
            tf = self.tmpf[h]
            sc.op("act", lambda e, out=out, tf=tf: e.activation(out=tf, in_=out, func=AF.Ln, scale=1.0, bias=float(D * EPS)),
                  reads=[pt], writes=[self.t_tmpf[h]])
            sc.op("act", lambda e, tf=tf, h=h: e.activation(out=self.rstd[:, h * HALF:(h + 1) * HALF], in_=tf, func=AF.Exp, scale=-0.5),
                  reads=[self.t_tmpf[h]], writes=[self.t_rstd])
        for h in halves:
            for k in range(KC):
                sl = slice(h * HALF, (h + 1) * HALF)
                sc.op("dve", lambda e, k=k, sl=sl: e.scalar_tensor_tensor(
                    out_ap[:, k, sl], self.xT[:, k, sl], self.normw[:, widx, k:k + 1], self.rstd[:, sl],
                    ALU.mult, ALU.mult),
                    reads=[self.t_xk[k], self.t_rstd, self.t_const], writes=[out_tok])

    def mlp_ple(self, li, ti):
        for _ in self.mlp_ple_g(li, ti):
            pass

    def mlp_ple_g(self, li, ti, halves=(0, 1), lin_banks=(0, 1), aux_banks=(2, 3), t_up=None, prefetch=True):
        sc = self.sc
        L = self.n_layers
        t0 = ti * T
        upT = self.big(BF16, [32, T], 0)
        if t_up is None:
            t_up = self.t_big0
        if prefetch:
            self.ple_prefetch(li, ti)
        pT, pwp, t_pT = self.pT, self.pwp, self.t_pT
        sq = self.big(BF16, [KC, T], 0)
        self.rmsnorm(L + li, self.hT, self.t_h, sq, t_up, halves=halves, banks=aux_banks)
        yield
        ei = [0]

        def evac_up(nch, h, ps, pt):
            j = ei[0] % 2
            ei[0] += 1
            tf = self.tmpf[j]
            sc.op("act", lambda e: e.activation(out=tf, in_=ps, func=AF.Relu), reads=[pt], writes=[self.t_tmpf[j]])
            sc.op("dve", lambda e: e.tensor_tensor(upT[:, nch, h * HALF:(h + 1) * HALF], tf, tf, ALU.mult),
                  reads=[self.t_tmpf[j]], writes=[t_up])
        yield from self.linear_fm_g(self.w_up[li], D, 0, DFF, self.hT, [self.t_h], evac_up, banks=lin_banks, halves=halves)
        xb = self.hT

        def evac_down(nch, h, ps, pt):
            sl = slice(h * HALF, (h + 1) * HALF)
            sc.op("dve", lambda e: e.tensor_tensor(self.xT[:, nch, sl], self.xT[:, nch, sl], ps, ALU.add),
                  reads=[pt, self.t_xk[nch]], writes=[self.t_xk[nch]])
            sc.op("act", lambda e: e.activation(out=xb[:, nch, sl], in_=self.xT[:, nch, sl], func=AF.Copy),
                  reads=[self.t_xk[nch]], writes=[self.t_h])
        yield from self.linear_fm_g(self.w_down[li], DFF, 0, D, upT, [t_up], evac_down, banks=lin_banks, halves=halves)
        gate_banks = lin_banks[:1] if len(aux_banks) == 0 or aux_banks[0] in lin_banks else lin_banks
        pp_banks = [b for b in aux_banks if b not in gate_banks] or [lin_banks[-1]]
        pctr = [0]

        def evac_gate(nch, h, ps, pt):
            j = ei[0] % 2
            ei[0] += 1
            tf = self.tmpf[j]
            sl = slice(h * HALF, (h + 1) * HALF)
            bp = pp_banks[pctr[0] % len(pp_banks)]
            pctr[0] += 1
            for k in range(2):
                sc.op("pe", lambda e, k=k, bp=bp: e.matmul(self.psf(bp), pwp[:, k, nch * 128:(nch + 1) * 128], pT[:, k, sl],
                                                           start=(k == 0), stop=(k == 1)),
                      reads=[t_pT], writes=[self.ps_tok[bp]])
            sc.op("act", lambda e: e.activation(out=tf, in_=ps, func=AF.Sigmoid), reads=[pt], writes=[self.t_tmpf[j]])
            sc.op("dve", lambda e, bp=bp: e.tensor_tensor(tf, tf, self.psf(bp), ALU.mult),
                  reads=[self.t_tmpf[j], self.ps_tok[bp]], writes=[self.t_tmpf[j]])
            sc.op("dve", lambda e: e.tensor_tensor(self.xT[:, nch, sl], self.xT[:, nch, sl], tf, ALU.add),
                  reads=[self.t_tmpf[j], self.t_xk[nch]], writes=[self.t_xk[nch]])
        yield from self.linear_fm_g(self.w_pgate[li], D, 0, D, xb, [self.t_h], evac_gate, banks=gate_banks, halves=halves)

    def ple_prefetch(self, li, ti):
        t0 = ti * T
        self.sc.dma("pool", "pld", lambda e: e.dma_start(
            out=self.pT, in_=self.pT_in[li].rearrange("(kc p) s -> p kc s", p=128)[:, :, t0:t0 + T]), writes=[self.t_pT])
        self.sc.dma("pool", "pwld", lambda e: e.dma_start(
            out=self.pwp, in_=self.w_pproj[li].rearrange("(kc p) n -> p kc n", p=128)), writes=[self.t_pT])

    def merge_deps(self, dst, srcs):
        for s_ in srcs:
            items = list(s_.r.items())
            if s_.w is not None:
                items.append(s_.w)
            for k, v in items:
                if v > dst.r.get(k, 0):
                    dst.r[k] = v

    def dbg(self, name, ap, tok, shape, dtype=F32):
        d = self.nc.dram_tensor("dbg_" + name, list(shape), dtype, kind="ExternalOutput").ap()
        self.sc.dma("sp", "dbg", lambda e: e.dma_start(out=d, in_=ap), reads=[tok], writes=[self.t_dbg])

    def gdn_layer_consts(self, li):
        sc = self.sc
        w = [self.t_lconst]
        sc.dma("sp", "lcst", lambda e: e.dma_start(out=self.convw, in_=self.d_convw[li]), writes=w)
        sc.dma("sp", "lcst", lambda e: e.dma_start(out=self.alog, in_=self.d_alog[li]), writes=w)
        sc.dma("sp", "lcst", lambda e: e.dma_start(out=self.dtb, in_=self.d_dtb[li]), writes=w)
        sc.dma("sp", "lcst", lambda e: e.dma_start(out=self.onw, in_=self.d_onw[li]), writes=w)
        sc.op("act", lambda e: e.activation(out=self.nexpA, in_=self.alog, func=AF.Exp), reads=w, writes=w)
        sc.op("dve", lambda e: e.tensor_scalar(self.nexpA, self.nexpA, -1.0, None, ALU.mult), reads=w, writes=w)
        sc.op("dve", lambda e: e.memset(self.halo, 0.0), writes=[self.t_halo])
        sc.op("dve", lambda e: e.memset(self.Sst, 0.0), writes=[self.t_S] + self.t_Sg)
        sc.op("dve", lambda e: e.memset(self.Sbf, 0.0), writes=[self.t_Sbf] + self.t_Sbfg)

    def gdn_tile(self, li, ti):
        sc = self.sc
        ai = li
        BIGC = 100.0
        qT = self.big(BF16, [KC, T], 0)
        kT = self.big(BF16, [KC, T], 16384)
        vT = self.big(BF16, [KC, T], 32768)
        zs = self.big(BF16, [KC, T], 49152)
        W0 = 65536
        sq = self.big(BF16, [KC, T], W0)
        qsq = self.big(BF16, [KC, T], W0)
        ksq = self.big(BF16, [KC, T], W0 + 16384)
        t_q, t_k, t_v, t_z, t_sq = Tok("q"), Tok("k"), Tok("v"), Tok("z"), Tok("sqw")
        self.merge_deps(t_q, [self.t_big0, self.t_big1])
        self.merge_deps(t_k, [self.t_big0, self.t_big1])
        self.merge_deps(t_v, [self.t_big0, self.t_big1])
        self.merge_deps(t_z, [self.t_big0, self.t_big1])
        self.merge_deps(t_sq, [self.t_big0, self.t_big1])
        CB = W0 + 20544
        pc = [self.big(BF16, [T + 8], CB + i * 2080) for i in range(2)]
        t_pc = [Tok("pc0"), Tok("pc1")]
        diag = [self.big(BF16, [4, 128], CB + 4160 + i * 1024) for i in range(2)]
        t_diag = [Tok("dg0"), Tok("dg1")]
        for t_ in t_pc + t_diag:
            self.merge_deps(t_, [self.t_big0, self.t_big1])

        self.ple_prefetch(li, ti)
        self.rmsnorm(li, self.hT, self.t_h, sq, t_sq)

        wv = self.a_w_in[ai].rearrange("(kc p) n -> p kc n", p=128)
        slab, stok = self.load_slab(wv[:, :, 4096:4112], KC, 16)
        ba = self.psf(4)[:, 0:128].rearrange("p (b c) -> p b c", b=8)
        for tb in range(8):
            for k in range(KC):
                sc.op("pe", lambda e, tb=tb, k=k: e.matmul(
                    self.psf(4)[:, tb * 16:(tb + 1) * 16], self.hT[:, k, tb * 128:(tb + 1) * 128], slab[:, k, :],
                    start=(k == 0), stop=(k == KC - 1)), reads=[stok, self.t_h], writes=[self.ps_tok[4]])
        SB, SG, SGC, SEG, SEGL, SCD, SLNK, SRK, SIRK, SSQ, S1, S2, S3, S4, ST0, ST1, ST2 = range(17)
        scal = self.scal
        ts = self.t_scal

        def sv(kind):
            return scal[:, kind, :]

        def sv3(kind):
            return scal[:, kind, :].rearrange("p (b c) -> p b c", b=8)
        p4 = self.ps_tok[4]
        sc.op("act", lambda e: e.activation(out=sv3(SB), in_=ba[:, :, 0:8], func=AF.Sigmoid), reads=[p4], writes=[ts])
        sc.op("dve", lambda e: e.tensor_tensor(sv3(ST0), ba[:, :, 8:16], self.dtb.rearrange("p (b c) -> p b c", b=8), ALU.add),
              reads=[p4, self.t_lconst], writes=[ts])
        sc.op("dve", lambda e: e.tensor_scalar(sv(ST2), sv(ST0), 0.0, None, ALU.max), reads=[ts], writes=[ts])
        sc.op("dve", lambda e: e.scalar_tensor_tensor(sv(ST1), sv(ST2), -2.0, sv(ST0), ALU.mult, ALU.add), reads=[ts], writes=[ts])
        sc.op("act", lambda e: e.activation(out=sv(ST1), in_=sv(ST1), func=AF.Exp), reads=[ts], writes=[ts])
        sc.op("act", lambda e: e.activation(out=sv(ST1), in_=sv(ST1), func=AF.Ln, bias=1.0), reads=[ts], writes=[ts])
        sc.op("dve", lambda e: e.tensor_tensor(sv(ST0), sv(ST2), sv(ST1), ALU.add), reads=[ts], writes=[ts])
        sc.op("dve", lambda e: e.tensor_tensor(sv(SG), sv(ST0), self.nexpA, ALU.mult), reads=[ts, self.t_lconst], writes=[ts])
        p5 = self.ps_tok[5]
        sb_ = self.scalb
        tsb = self.t_scalb
        GH, GL, L4H, L4L, LRH, LRL = range(6)
        sc.op("act", lambda e: e.activation(out=sb_[:, GH, :], in_=sv(SG), func=AF.Copy), reads=[ts], writes=[tsb])
        sc.op("dve", lambda e: e.tensor_tensor(sb_[:, GL, :], sv(SG), sb_[:, GH, :], ALU.subtract), reads=[ts, tsb], writes=[tsb])
        sc.op("pe", lambda e: e.matmul(self.psf(5)[:, 0:64], self.tri_bf, sb_[:, GH, :], start=True, stop=False),
              reads=[tsb, self.t_const], writes=[p5])
        sc.op("pe", lambda e: e.matmul(self.psf(5)[:, 0:64], self.tri_bf, sb_[:, GL, :], start=False, stop=True),
              reads=[tsb, self.t_const], writes=[p5])
        sc.op("pe", lambda e: e.matmul(self.psf(5)[:, 64:128], self.ones, sb_[:, GH, :], start=True, stop=False),
              reads=[tsb, self.t_const], writes=[p5])
        sc.op("pe", lambda e: e.matmul(self.psf(5)[:, 64:128], self.ones, sb_[:, GL, :], start=False, stop=True),
              reads=[tsb, self.t_const], writes=[p5])
        sc.op("act", lambda e: e.activation(out=sv(SGC), in_=self.psf(5)[:, 0:64], func=AF.Copy), reads=[p5], writes=[ts])
        sc.op("act", lambda e: e.activation(out=sv(SEG), in_=self.psf(5)[:, 0:64], func=AF.Exp), reads=[p5], writes=[ts])
        sc.op("act", lambda e: e.activation(out=sv(SCD), in_=self.psf(5)[:, 64:128], func=AF.Exp), reads=[p5], writes=[ts])
        sc.op("dve", lambda e: e.tensor_tensor(sv(SEGL), self.psf(5)[:, 64:128], sv(SGC), ALU.subtract), reads=[p5, ts], writes=[ts])
        sc.op("act", lambda e: e.activation(out=sv(SEGL), in_=sv(SEGL), func=AF.Exp), reads=[ts], writes=[ts])
        dests = [qT, kT, vT]
        dtoks = [t_q, t_k, t_v]
        cstate = {}

        cdef = [None]

        def conv_tail(nch, par):
            pcb, tp = pc[par], t_pc[par]
            dst = dests[nch // 8]
            dtok = dtoks[nch // 8]
            for hh in range(2):
                b = 2 + hh
                for j in range(4):
                    sc.op("pe", lambda e, b=b, j=j, hh=hh: e.matmul(
                        self.psf(b), diag[par][:, j, :], pcb[:, j + hh * HALF:j + (hh + 1) * HALF],
                        start=(j == 0), stop=(j == 3)), reads=[tp, t_diag[par]], writes=[self.ps_tok[b]])
                sc.op("act", lambda e, b=b, hh=hh: e.activation(
                    out=dst[:, nch % 8, hh * HALF:(hh + 1) * HALF], in_=self.psf(b), func=AF.Silu),
                    reads=[self.ps_tok[b]], writes=[dtok])

        def evac_qkv(nch, h, ps, pt):
            par = nch % 2
            pcb, tp = pc[par], t_pc[par]
            if h == 0:
                sc.op("dve", lambda e: e.tensor_copy(pcb[:, 0:3], self.halo[:, nch, :]), reads=[self.t_halo], writes=[tp])
                for j in range(4):
                    sc.op("dve", lambda e, j=j: e.tensor_scalar(diag[par][:, j, :], self.ident, self.convw[:, j, nch:nch + 1], None, ALU.mult),
                          reads=[self.t_const, self.t_lconst], writes=[t_diag[par]])
            sc.op("act", lambda e: e.activation(out=pcb[:, 3 + h * HALF:3 + (h + 1) * HALF], in_=ps, func=AF.Copy),
                  reads=[pt], writes=[tp])
            if h == 0 and cdef[0] is not None:
                cdef[0]()
                cdef[0] = None
            if h == 1:
                sc.op("dve", lambda e: e.tensor_copy(self.halo[:, nch, :], pcb[:, T:T + 3]), reads=[tp], writes=[self.t_halo])
                cdef[0] = lambda: conv_tail(nch, par)
        self.linear_fm(self.a_w_in[ai], D, 0, 3072, self.hT, [self.t_h], evac_qkv)
        if cdef[0] is not None:
            cdef[0]()
            cdef[0] = None

        def evac_z(nch, h, ps, pt):
            sc.op("act", lambda e: e.activation(out=zs[:, nch, h * HALF:(h + 1) * HALF], in_=ps, func=AF.Silu),
                  reads=[pt], writes=[t_z])
        self.linear_fm(self.a_w_in[ai], D, 3072, 4096, self.hT, [self.t_h], evac_z)

        if getattr(self, "stage", 99) < 2:
            return
        for k in range(KC):
            sc.op("act", lambda e, k=k: e.activation(out=qsq[:, k, :], in_=qT[:, k, :], func=AF.Square), reads=[t_q], writes=[t_sq])
            sc.op("dve", lambda e, k=k: e.tensor_tensor(ksq[:, k, :], kT[:, k, :], kT[:, k, :], ALU.mult), reads=[t_k], writes=[t_sq])
        p6 = self.ps_tok[6]
        for cb in range(8):
            for h in range(8):
                col = cb * 8 + h
                sc.op("pe", lambda e, cb=cb, h=h, col=col: e.matmul(
                    self.psf(6)[:, col:col + 1], qsq[:, h, cb * 128:(cb + 1) * 128], self.ones[:, 0:1], start=True, stop=True),
                    reads=[t_sq, self.t_const], writes=[p6])
                sc.op("pe", lambda e, cb=cb, h=h, col=col: e.matmul(
                    self.psf(6)[:, 64 + col:65 + col], ksq[:, h, cb * 128:(cb + 1) * 128], self.ones[:, 0:1], start=True, stop=True),
                    reads=[t_sq, self.t_const], writes=[p6])
        sc.op("act", lambda e: e.activation(out=sv(SLNK), in_=self.psf(6)[:, 64:128], func=AF.Ln, bias=EPS), reads=[p6], writes=[ts])
        sc.op("act", lambda e: e.activation(out=sv(SRK), in_=sv(SLNK), func=AF.Exp, scale=-0.5), reads=[ts], writes=[ts])
        sc.op("act", lambda e: e.activation(out=sv(SIRK), in_=sv(SLNK), func=AF.Exp, scale=0.5), reads=[ts], writes=[ts])
        sc.op("act", lambda e: e.activation(out=sv(SSQ), in_=self.psf(6)[:, 0:64], func=AF.Ln, bias=EPS), reads=[p6], writes=[ts])
        sc.op("act", lambda e: e.activation(out=sv(SSQ), in_=sv(SSQ), func=AF.Exp, scale=-0.5), reads=[ts], writes=[ts])
        sc.op("dve", lambda e: e.tensor_scalar(sv(SSQ), sv(SSQ), float(128 ** -0.5), None, ALU.mult), reads=[ts], writes=[ts])
        SL4, SLRK = S4, ST1
        sc.op("dve", lambda e: e.tensor_tensor(sv(S3), sv(SRK), sv(SB), ALU.mult), reads=[ts], writes=[ts])
        sc.op("dve", lambda e: e.tensor_tensor(sv(ST0), sv(S3), sv(SRK), ALU.mult), reads=[ts], writes=[ts])
        sc.op("dve", lambda e: e.tensor_tensor(sv(S1), sv(ST0), sv(SEG), ALU.mult), reads=[ts], writes=[ts])
        sc.op("dve", lambda e: e.tensor_tensor(sv(S2), sv(SRK), sv(SEGL), ALU.mult), reads=[ts], writes=[ts])
        sc.op("act", lambda e: e.activation(out=sv(ST2), in_=sv(SB), func=AF.Ln), reads=[ts], writes=[ts])
        sc.op("dve", lambda e: e.tensor_tensor(sv(SL4), sv(ST2), sv(SLNK), ALU.subtract), reads=[ts], writes=[ts])
        sc.op("dve", lambda e: e.tensor_scalar(sv(SL4), sv(SL4), -80.0, None, ALU.max), reads=[ts], writes=[ts])
        sc.op("dve", lambda e: e.tensor_scalar(sv(SLRK), sv(SLNK), -0.5, None, ALU.mult), reads=[ts], writes=[ts])
        sc.op("act", lambda e: e.activation(out=sb_[:, L4H, :], in_=sv(SL4), func=AF.Copy), reads=[ts], writes=[tsb])
        sc.op("dve", lambda e: e.tensor_tensor(sb_[:, L4L, :], sv(SL4), sb_[:, L4H, :], ALU.subtract), reads=[ts, tsb], writes=[tsb])
        sc.op("act", lambda e: e.activation(out=sb_[:, LRH, :], in_=sv(SLRK), func=AF.Copy), reads=[ts], writes=[tsb])
        sc.op("dve", lambda e: e.tensor_tensor(sb_[:, LRL, :], sv(SLRK), sb_[:, LRH, :], ALU.subtract), reads=[ts, tsb], writes=[tsb])

        if getattr(self, "stage", 99) < 3:
            return
        names = ["kbg", "kd", "vb", "Gm", "E", "Pa", "Pta", "Pb", "Ptb", "Rm", "attnT", "nwT", "vnew", "avsb", "opp", "onb", "osc"]
        names = names + ["GmL"]
        Ws, wts = [], []
        wo = [W0]

        def wb(dtype, shape):
            esz = 2 if dtype == BF16 else 4
            n = esz
            for s_ in shape:
                n *= s_
            ap = self.big(dtype, shape, wo[0])
            wo[0] += (n + 31) // 32 * 32
            assert wo[0] <= self.big_n, wo[0]
            return ap
        NHT = getattr(self, "gdn_nht", 2)
        NTH = 8 // NHT
        for hg_ in range(NTH):
            W = {}
            for n_ in ["kbg", "kd", "vb", "Pa", "Pta", "Pb", "Ptb", "Rm", "attnT", "nwT", "vnew", "onb", "Gm", "GmL"]:
                W[n_] = wb(BF16, [NHT, 128])
            for n_ in ["E", "avsb", "opp"]:
                W[n_] = wb(F32, [NHT, 128])
            W["osc"] = wb(F32, [2 * NHT])
            wt = {n_: Tok(n_ + str(hg_)) for n_ in names}
            for n_ in names:
                self.merge_deps(wt[n_], [t_sq] + t_pc + t_diag)
                wt[n_].r["pe"] = sc.cnt["pe"]
            Ws.append(W)
            wts.append(wt)
        og = self.hT
        t_ogs = [Tok("og%d" % i_) for i_ in range(NTH)]
        if len(self.t_Sg) != NTH:
            self.t_Sg = [Tok("S%d" % i_) for i_ in range(NTH)]
            self.t_Sbfg = [Tok("Sbf%d" % i_) for i_ in range(NTH)]
            for t_ in self.t_Sg:
                self.merge_deps(t_, [self.t_S])
                t_.w = self.t_S.w
            for t_ in self.t_Sbfg:
                self.merge_deps(t_, [self.t_Sbf])
                t_.w = self.t_Sbf.w
        for t_ in t_ogs:
            self.merge_deps(t_, [self.t_h])
        ps_t = self.ps_tok
        bank_ctr = [0] * NTH
        OVERLAP = getattr(self, "gdn_overlap", False)

        def body(cb, hg):
            W, wt = Ws[hg], wts[hg]
            kbg, kd, vb, Gm, E, Rm = W["kbg"], W["kd"], W["vb"], W["Gm"], W["E"], W["Rm"]
            GmL = W["GmL"]
            sb_ = self.scalb
            tsb = self.t_scalb

            def cb1(kind, hh):
                c = col(hh)
                return sb_[:, kind, c:c + 1]
            Pa, Pta, Pb, Ptb = W["Pa"], W["Pta"], W["Pb"], W["Ptb"]
            attnT, nwT, vnew, avsb, opp, onb, osc = W["attnT"], W["nwT"], W["vnew"], W["avsb"], W["opp"], W["onb"], W["osc"]
            t_S, t_Sbf, t_og = self.t_Sg[hg], self.t_Sbfg[hg], t_ogs[hg]
            tsl = slice(cb * 128, (cb + 1) * 128)

            SW = NHT * 128

            def nb():
                if NHT == 4:
                    nbk = 3 if OVERLAP else 4
                    s_ = bank_ctr[hg] % nbk
                    bank_ctr[hg] += 1
                    return (hg * nbk + s_, 0)
                if OVERLAP:
                    s_ = bank_ctr[hg] % 3
                    bank_ctr[hg] += 1
                    slot = hg * 3 + s_
                    return (slot // 2, (slot % 2) * SW)
                s_ = bank_ctr[hg] % 4
                bank_ctr[hg] += 1
                return (hg * 2 + s_ // 2, (s_ % 2) * SW)

            def pf(sl_, hh=None):
                b_, o_ = sl_
                if hh is None:
                    return self.psf(b_)[:, o_:o_ + SW]
                return self.psf(b_)[:, o_ + hh * 128:o_ + (hh + 1) * 128]

            def pb(sl_, hh=None, second=False):
                b_, o_ = sl_
                base = 2 * o_ + (SW if second else 0)
                if hh is None:
                    return self.psb(b_)[:, base:base + SW]
                return self.psb(b_)[:, base + hh * 128:base + (hh + 1) * 128]

            def pt_(sl_):
                return ps_t[sl_[0]]

            def col(hh):
                return cb * 8 + hg * NHT + hh

            def c1(kind, hh):
                c = col(hh)
                return scal[:, kind, c:c + 1]

            def f4(sl_):
                return pf(sl_).rearrange("p (a b) -> p a b", a=NHT)
            H = [hg * NHT + hh for hh in range(NHT)]
            R4 = range(NHT)
            b_tr = nb()
            for hh in R4:
                sc.op("pe", lambda e, hh=hh: e.transpose(pb(b_tr, hh), kT[:, H[hh], tsl], self.ident),
                      reads=[t_k, self.t_const], writes=[pt_(b_tr)])
                sc.op("pe", lambda e, hh=hh: e.transpose(pb(b_tr, hh, True), vT[:, H[hh], tsl], self.ident),
                      reads=[t_v, self.t_const], writes=[pt_(b_tr)])
            yield
            for hh in R4:
                sc.op("act", lambda e, hh=hh: e.activation(out=kbg[:, hh, :], in_=pb(b_tr, hh), func=AF.Copy, scale=c1(S1, hh)),
                      reads=[pt_(b_tr), ts], writes=[wt["kbg"]])
                sc.op("dve", lambda e, hh=hh: e.tensor_scalar(kd[:, hh, :], pb(b_tr, hh), c1(S2, hh), None, ALU.mult),
                      reads=[pt_(b_tr), ts], writes=[wt["kd"]])
                sc.op("act", lambda e, hh=hh: e.activation(out=vb[:, hh, :], in_=pb(b_tr, hh, True), func=AF.Copy, scale=c1(S3, hh)),
                      reads=[pt_(b_tr), ts], writes=[wt["vb"]])
            b_kk = nb()
            b_d = nb()
            for hh in R4:
                sc.op("pe", lambda e, hh=hh: e.matmul(pf(b_kk, hh), kT[:, H[hh], tsl], kT[:, H[hh], tsl], start=True, stop=True),
                      reads=[t_k], writes=[pt_(b_kk)])
                sc.op("dve", lambda e, hh=hh: e.tensor_scalar(Gm[:, hh, :], self.maskS_bf, cb1(0, hh), None, ALU.mult),
                      reads=[tsb, self.t_const], writes=[wt["Gm"]])
                sc.op("dve", lambda e, hh=hh: e.tensor_scalar(GmL[:, hh, :], self.maskS_bf, cb1(1, hh), None, ALU.mult),
                      reads=[tsb, self.t_const], writes=[wt["GmL"]])
            for hh in R4:
                o_ = pf(b_d, hh)
                sc.op("pe", lambda e, hh=hh, o_=o_: e.matmul(o_, self.tri_bf, Gm[:, hh, :], start=True, stop=False),
                      reads=[wt["Gm"], self.t_const], writes=[pt_(b_d)])
                sc.op("pe", lambda e, hh=hh, o_=o_: e.matmul(o_, self.tri_bf, GmL[:, hh, :], start=False, stop=False),
                      reads=[wt["GmL"], self.t_const], writes=[pt_(b_d)])
                sc.op("pe", lambda e, hh=hh, o_=o_: e.matmul(o_, self.ident, cb1(2, hh).to_broadcast([128, 128]), start=False, stop=False),
                      reads=[tsb, self.t_const], writes=[pt_(b_d)])
                sc.op("pe", lambda e, hh=hh, o_=o_: e.matmul(o_, self.ident, cb1(3, hh).to_broadcast([128, 128]), start=False, stop=False),
                      reads=[tsb, self.t_const], writes=[pt_(b_d)])
                sc.op("pe", lambda e, hh=hh, o_=o_: e.matmul(o_, self.ident, self.C1_bf, start=False, stop=True),
                      reads=[self.t_const], writes=[pt_(b_d)])
            yield
            sc.op("act", lambda e: e.activation(out=E, in_=f4(b_d), func=AF.Exp, bias=-BIGC), reads=[pt_(b_d)], writes=[wt["E"]])
            sc.op("dve", lambda e: e.tensor_tensor(Pta, f4(b_kk), E, ALU.mult), reads=[pt_(b_kk), wt["E"]], writes=[wt["Pta"]])
            yield
            b_p = nb()
            for hh in R4:
                sc.op("pe", lambda e, hh=hh: e.transpose(pb(b_p, hh), Pta[:, hh, :], self.ident),
                      reads=[wt["Pta"], self.t_const], writes=[pt_(b_p)])
            yield
            sc.op("act", lambda e: e.activation(out=Pa, in_=pb(b_p).rearrange("p (a b) -> p a b", a=NHT), func=AF.Copy),
                  reads=[pt_(b_p)], writes=[wt["Pa"]])
            sc.op("dve", lambda e: e.tensor_tensor(Rm, self.ident.unsqueeze(1).to_broadcast([128, NHT, 128]),
                                                   pb(b_p).rearrange("p (a b) -> p a b", a=NHT), ALU.subtract),
                  reads=[pt_(b_p), self.t_const], writes=[wt["Rm"]])
            cur = (Pa, Pta, "Pa", "Pta")
            nxt = (Pb, Ptb, "Pb", "Ptb")
            for lvl in range(6):
                P_, Pt_, nP, nPt = cur
                Pn, Ptn, nPn, nPtn = nxt
                last = (lvl == 5)
                b_qt = nb()
                for hh in R4:
                    sc.op("pe", lambda e, hh=hh, P_=P_, Pt_=Pt_, b_qt=b_qt: e.matmul(pf(b_qt, hh), P_[:, hh, :], Pt_[:, hh, :], start=True, stop=True),
                          reads=[wt[nP], wt[nPt]], writes=[pt_(b_qt)])
                if not last:
                    b_q = nb()
                    for hh in R4:
                        sc.op("pe", lambda e, hh=hh, P_=P_, Pt_=Pt_, b_q=b_q: e.matmul(pf(b_q, hh), Pt_[:, hh, :], P_[:, hh, :], start=True, stop=True),
                              reads=[wt[nP], wt[nPt]], writes=[pt_(b_q)])
                yield
                sc.op("act", lambda e, Ptn=Ptn, b_qt=b_qt: e.activation(out=Ptn, in_=f4(b_qt), func=AF.Copy),
                      reads=[pt_(b_qt)], writes=[wt[nPtn]])
                if not last:
                    sc.op("dve", lambda e, Pn=Pn, b_q=b_q: e.tensor_copy(Pn, f4(b_q)), reads=[pt_(b_q)], writes=[wt[nPn]])
                b_r = nb()
                for hh in R4:
                    sc.op("pe", lambda e, hh=hh, Ptn=Ptn, b_r=b_r: e.matmul(pf(b_r, hh), Ptn[:, hh, :], Rm[:, hh, :], start=True, stop=True),
                          reads=[wt[nPtn], wt["Rm"]], writes=[pt_(b_r)])
                yield
                sc.op("dve", lambda e, b_r=b_r: e.tensor_tensor(Rm, Rm, f4(b_r), ALU.add),
                      reads=[pt_(b_r), wt["Rm"]], writes=[wt["Rm"]])
                cur, nxt = nxt, cur
            b_d2 = nb()
            for hh in R4:
                o_ = pf(b_d2, hh)
                sc.op("pe", lambda e, hh=hh, o_=o_: e.matmul(o_, Gm[:, hh, :], self.tri_bf, start=True, stop=False),
                      reads=[wt["Gm"], self.t_const], writes=[pt_(b_d2)])
                sc.op("pe", lambda e, hh=hh, o_=o_: e.matmul(o_, GmL[:, hh, :], self.tri_bf, start=False, stop=False),
                      reads=[wt["GmL"], self.t_const], writes=[pt_(b_d2)])
                sc.op("pe", lambda e, hh=hh, o_=o_: e.matmul(o_, self.ident, cb1(4, hh).to_broadcast([128, 128]), start=False, stop=False),
                      reads=[tsb, self.t_const], writes=[pt_(b_d2)])
                sc.op("pe", lambda e, hh=hh, o_=o_: e.matmul(o_, self.ident, cb1(5, hh).to_broadcast([128, 128]), start=False, stop=False),
                      reads=[tsb, self.t_const], writes=[pt_(b_d2)])
                sc.op("pe", lambda e, hh=hh, o_=o_: e.matmul(o_, self.ident, self.C2_bf, start=False, stop=True),
                      reads=[self.t_const], writes=[pt_(b_d2)])
            b_kq = nb()
            for hh in R4:
                sc.op("pe", lambda e, hh=hh: e.matmul(pf(b_kq, hh), kT[:, H[hh], tsl], qT[:, H[hh], tsl], start=True, stop=True),
                      reads=[t_k, t_q], writes=[pt_(b_kq)])
            b_w = nb()
            for hh in R4:
                sc.op("pe", lambda e, hh=hh: e.matmul(pf(b_w, hh), kbg[:, hh, :], Rm[:, hh, :], start=True, stop=True),
                      reads=[wt["kbg"], wt["Rm"]], writes=[pt_(b_w)])
            yield
            sc.op("act", lambda e: e.activation(out=E, in_=f4(b_d2), func=AF.Exp, bias=-BIGC), reads=[pt_(b_d2)], writes=[wt["E"]])
            sc.op("act", lambda e: e.activation(out=nwT, in_=f4(b_w), func=AF.Copy, scale=-1.0), reads=[pt_(b_w)], writes=[wt["nwT"]])
            sc.op("dve", lambda e: e.tensor_tensor(attnT, f4(b_kq), E, ALU.mult), reads=[pt_(b_kq), wt["E"]], writes=[wt["attnT"]])
            yield
            b_v = nb()
            for hh in R4:
                sc.op("pe", lambda e, hh=hh: e.matmul(pf(b_v, hh), Rm[:, hh, :], vb[:, hh, :], start=True, stop=False),
                      reads=[wt["Rm"], wt["vb"]], writes=[pt_(b_v)])
                sc.op("pe", lambda e, hh=hh: e.matmul(pf(b_v, hh), nwT[:, hh, :], self.Sbf[:, H[hh], :], start=False, stop=True),
                      reads=[wt["nwT"], t_Sbf], writes=[pt_(b_v)])
            b_qs = nb()
            for hh in R4:
                sc.op("pe", lambda e, hh=hh: e.matmul(pf(b_qs, hh), qT[:, H[hh], tsl], self.Sbf[:, H[hh], :], start=True, stop=True),
                      reads=[t_q, t_Sbf], writes=[pt_(b_qs)])
            yield
            for hh in R4:
                sc.op("act", lambda e, hh=hh: e.activation(out=vnew[:, hh, :], in_=pf(b_v, hh), func=AF.Copy, scale=c1(SIRK, hh)),
                      reads=[pt_(b_v), ts], writes=[wt["vnew"]])
            yield
            b_av = nb()
            for hh in R4:
                sc.op("pe", lambda e, hh=hh: e.matmul(pf(b_av, hh), attnT[:, hh, :], vnew[:, hh, :], start=True, stop=True),
                      reads=[wt["attnT"], wt["vnew"]], writes=[pt_(b_av)])
            b_s = nb()
            for hh in R4:
                sc.op("pe", lambda e, hh=hh: e.matmul(pf(b_s, hh), kd[:, hh, :], vnew[:, hh, :], start=True, stop=True),
                      reads=[wt["kd"], wt["vnew"]], writes=[pt_(b_s)])
            yield
            for hh in R4:
                sc.op("dve", lambda e, hh=hh: e.scalar_tensor_tensor(self.Sst[:, H[hh], :], self.Sst[:, H[hh], :], c1(SCD, hh), pf(b_s, hh), ALU.mult, ALU.add),
                      reads=[pt_(b_s), ts, t_S], writes=[t_S])
            sc.op("act", lambda e: e.activation(out=self.Sbf[:, hg * NHT:(hg + 1) * NHT, :], in_=self.Sst[:, hg * NHT:(hg + 1) * NHT, :], func=AF.Copy),
                  reads=[t_S], writes=[t_Sbf])
            sc.op("act", lambda e: e.activation(out=avsb, in_=f4(b_av), func=AF.Copy), reads=[pt_(b_av)], writes=[wt["avsb"]])
            for hh in R4:
                sc.op("dve", lambda e, hh=hh: e.scalar_tensor_tensor(opp[:, hh, :], pf(b_qs, hh), c1(SEG, hh), avsb[:, hh, :], ALU.mult, ALU.add),
                      reads=[pt_(b_qs), ts, wt["avsb"]], writes=[wt["opp"]])
            yield
            sc.op("act", lambda e: e.activation(out=avsb, in_=opp, func=AF.Square), reads=[wt["opp"]], writes=[wt["avsb"]])
            sc.op("dve", lambda e: e.tensor_reduce(osc[:, 0:NHT], avsb, mybir.AxisListType.X, ALU.add), reads=[wt["avsb"]], writes=[wt["osc"]])
            c0 = col(0)
            sqv = scal[:, SSQ, c0:c0 + NHT]
            sc.op("dve", lambda e: e.tensor_tensor(osc[:, NHT:2 * NHT], sqv, sqv, ALU.mult), reads=[ts, wt["osc"]], writes=[wt["osc"]])
            sc.op("dve", lambda e: e.tensor_tensor(osc[:, NHT:2 * NHT], osc[:, NHT:2 * NHT], osc[:, 0:NHT], ALU.mult), reads=[wt["osc"]], writes=[wt["osc"]])
            sc.op("act", lambda e: e.activation(out=osc[:, NHT:2 * NHT], in_=osc[:, NHT:2 * NHT], func=AF.Ln, scale=1.0 / 128, bias=EPS), reads=[wt["osc"]], writes=[wt["osc"]])
            sc.op("act", lambda e: e.activation(out=osc[:, NHT:2 * NHT], in_=osc[:, NHT:2 * NHT], func=AF.Exp, scale=-0.5), reads=[wt["osc"]], writes=[wt["osc"]])
            sc.op("dve", lambda e: e.tensor_tensor(osc[:, NHT:2 * NHT], osc[:, NHT:2 * NHT], sqv, ALU.mult), reads=[ts, wt["osc"]], writes=[wt["osc"]])
            yield
            for hh in R4:
                sc.op("act", lambda e, hh=hh: e.activation(out=onb[:, hh, :], in_=opp[:, hh, :], func=AF.Copy, scale=osc[:, NHT + hh:NHT + hh + 1]),
                      reads=[wt["opp"], wt["osc"]], writes=[wt["onb"]])
            b_t = nb()
            for hh in R4:
                sc.op("pe", lambda e, hh=hh: e.transpose(pb(b_t, hh), onb[:, hh, :], self.ident),
                      reads=[wt["onb"], self.t_const], writes=[pt_(b_t)])
            yield
            sc.op("dve", lambda e: e.scalar_tensor_tensor(og[:, hg * NHT:(hg + 1) * NHT, tsl], pb(b_t).rearrange("p (a b) -> p a b", a=NHT),
                                                          self.onw[:, 0:1], zs[:, hg * NHT:(hg + 1) * NHT, tsl], ALU.mult, ALU.mult),
                  reads=[pt_(b_t), self.t_lconst, t_z], writes=[t_og])

        prog = [0] * NTH

        def hthread(hg):
            for cb in range(8):
                yield from body(cb, hg)
                prog[hg] = cb + 1

        def evac_o(nch, h, ps, pt):
            sl = slice(h * HALF, (h + 1) * HALF)
            sc.op("dve", lambda e: e.tensor_tensor(self.xT[:, nch, sl], self.xT[:, nch, sl], ps, ALU.add),
                  reads=[pt, self.t_xk[nch]], writes=[self.t_xk[nch]])
        t_up_h = [Tok("up_h0"), Tok("up_h1")]

        def post_g(h, lin_banks, aux_banks):
            hs = (h,) if isinstance(h, int) else tuple(h)
            tu = t_up_h[hs[0]]
            self.merge_deps(tu, [t_q, t_k, t_v, t_z, t_sq])
            yield from self.linear_fm_g(self.a_w_out[ai], D, 0, D, og, t_ogs, evac_o, banks=lin_banks, halves=hs)
            yield from self.mlp_ple_g(li, ti, halves=hs, lin_banks=lin_banks, aux_banks=aux_banks, t_up=tu, prefetch=False)

        ths = [hthread(i_) for i_ in range(NTH)]
        alive = [True] * NTH
        step = 0
        DELAY = getattr(self, "gdn_delay", 3)
        post0 = None
        post0_alive = False
        while any(alive):
            for i_, th in enumerate(ths):
                if not alive[i_]:
                    continue
                if step < DELAY * i_:
                    continue
                try:
                    next(th)
                except StopIteration:
                    alive[i_] = False
            if OVERLAP and post0 is None and min(prog) >= 4:
                post0 = post_g(0, (6, 7), (6,))
                post0_alive = True
            if post0_alive:
                try:
                    next(post0)
                except StopIteration:
                    post0_alive = False
            step += 1
        if OVERLAP:
            if post0 is None:
                post0 = post_g(0, (6, 7), (6,))
            for _ in post0:
                pass
            for _ in post_g(1, (0, 1), (2, 3)):
                pass
        else:
            for _ in post_g((0, 1), (0, 1), (2, 3)):
                pass
        allt = [t_q, t_k, t_v, t_z, t_sq] + t_pc + t_diag + [wts[g_][n_] for g_ in range(len(wts)) for n_ in names] + t_ogs + t_up_h
        self.merge_deps(self.t_big0, allt)
        self.merge_deps(self.t_big1, allt)

    def attn_consts_np(self):
        S = self.S
        inv = np.power(np.float32(500000.0), -np.arange(0, 32, 2, dtype=np.float32) / np.float32(32)).astype(np.float32)
        ang = (np.arange(S, dtype=np.float32)[:, None] * inv[None, :]).astype(np.float32)
        cos = np.cos(ang).astype(np.float32)
        sin = np.sin(ang).astype(np.float32)
        C32 = np.concatenate([cos.T, cos.T], axis=0)
        S32 = np.concatenate([sin.T, sin.T], axis=0)
        prot = np.zeros((128, 128), np.float32)
        for m_ in range(16):
            prot[m_ + 16, m_] = -1.0
            prot[m_, m_ + 16] = 1.0
        kj = np.arange(128)[:, None]

        def mk(delta, nq):
            qi = np.arange(nq)[None, :]
            dist = qi + delta - kj
            return np.where((dist >= 0) & (dist <= 128), 0.0, NEG).astype(np.float32)
        masks = {}
        masks[(128, 128)] = mk(128, 128)
        masks[(0, 256)] = mk(0, 256)
        for dl in (0, 64, 128, 192):
            masks[(dl, 64)] = mk(dl, 64)
        return C32, S32, prot, masks

    def setup_attn_consts(self):
        sc = self.sc
        C32, S32, prot, masks = self.attn_consts_np()
        self.d_cos = self.add_const("c_cos", C32)
        self.d_sin = self.add_const("c_sin", S32)
        d_prot = self.add_const("c_prot", prot.astype(ml_dtypes.bfloat16))
        cw = [self.t_const, self.t_S, self.t_Sbf] + self.t_Sg + self.t_Sbfg
        sc.dma("sp", "cst", lambda e: e.dma_start(out=self.protT, in_=d_prot), writes=cw)
        self.mask_ap = {}
        off = 0
        for key, arr in masks.items():
            d_ = self.add_const("c_mask_%d_%d" % key, arr.astype(ml_dtypes.bfloat16))
            dst = self.maskbuf[:, off:off + key[1]]
            off += key[1]
            self.mask_ap[key] = dst
            sc.dma("sp", "cst", lambda e, d_=d_, dst=dst: e.dma_start(out=dst, in_=d_), writes=cw)

    def kv_phase(self, src, src_toks):
        sc = self.sc
        S = self.S
        ST = min(2048, S)
        L = self.n_layers
        GD = (1, 4, 16)
        hT2 = self.big(BF16, [KC, ST], 0)
        t_h2 = Tok("h2")
        sqs = self.big(BF16, [KC, T], 32768)
        t_sqs = Tok("sqs")
        cs = self.big(F32, [2, ST], 49152)
        t_cs = Tok("cs")
        knat = [self.big(BF16, [HALF], 65536 + i * 1024) for i in range(2)]
        t_knat = [Tok("knat0"), Tok("knat1")]
        kperm = [self.big(BF16, [ST], 67584 + i * 4096) for i in range(2)]
        t_kperm = [Tok("kperm0"), Tok("kperm1")]
        rt = self.tmpf
        t_rt = self.t_tmpf
        vtmp = [self.big(BF16, [256], 79872 + i * 512) for i in range(4)]
        t_vtmp = [Tok("vt%d" % i) for i in range(4)]
        allt = [t_h2, t_sqs, t_cs] + t_knat + t_kperm + t_vtmp
        for t_ in allt:
            self.merge_deps(t_, [self.t_big0, self.t_big1])
        wkv = self.b_w_kv
        for st in range(S // ST):
            for half in range(ST // T):
                ti = st * (ST // T) + half
                self.load_x(src, src_toks[ti] if src_toks is not None else None, ti)
                self.rmsnorm(2 * L + 1, hT2[:, :, half * T:(half + 1) * T], t_h2, sqs, t_sqs)
            sc.dma("sp", "csld", lambda e, st=st: e.dma_start(out=cs[0:32, 0, :], in_=self.d_cos[:, st * ST:(st + 1) * ST]), writes=[t_cs])
            sc.dma("sp", "csld", lambda e, st=st: e.dma_start(out=cs[0:32, 1, :], in_=self.d_sin[:, st * ST:(st + 1) * ST]), writes=[t_cs])
            nh = ST // HALF

            kdef = [None]
            kctr = [0]

            def k_tail(nch, h, par, st=st):
                g = nch // 2
                d = GD[g]
                kn, tkn = knat[par], t_knat[par]
                kp, tkp = kperm[nch % 2], t_kperm[nch % 2]
                r_, tr_ = rt[par], t_rt[par]
                sl = slice(h * HALF, (h + 1) * HALF)
                b = 2 + par
                sc.op("pe", lambda e: e.matmul(self.psf(b), self.protT, kn, start=True, stop=True),
                      reads=[tkn, self.t_const], writes=[self.ps_tok[b]])
                sc.op("dve", lambda e: e.tensor_tensor(r_[0:32, :], self.psf(b)[0:32, :], cs[0:32, 1, sl], ALU.mult),
                      reads=[self.ps_tok[b], t_cs], writes=[tr_])
                sc.op("dve", lambda e: e.tensor_tensor(kn[0:32, :], kn[0:32, :], cs[0:32, 0, sl], ALU.mult),
                      reads=[t_cs, tkn], writes=[tkn])
                sc.op("dve", lambda e: e.tensor_tensor(kn[0:32, :], kn[0:32, :], r_[0:32, :], ALU.add),
                      reads=[tr_, tkn], writes=[tkn])
                jn = HALF // d
                dst = kp.rearrange("p (r j) -> p r j", r=d)[:, :, h * jn:(h + 1) * jn]
                srcv = kn.rearrange("p (j r) -> p r j", r=d)
                sc.op("act", lambda e: e.activation(out=dst, in_=srcv, func=AF.Copy), reads=[tkn], writes=[tkp])
                if h == nh - 1:
                    jt = ST // d
                    dd = self.K_dram[nch].rearrange("p (r j) -> p r j", r=d)[:, :, st * jt:(st + 1) * jt]
                    sc.dma("sp", "kst%d" % (nch % 2), lambda e: e.dma_start(out=dd, in_=kp.rearrange("p (r j) -> p r j", r=d)),
                           reads=[tkp], writes=[self.t_kdram2[nch % 2]])

            def evac_k(nch, h, ps, pt):
                par = kctr[0] % 2
                kctr[0] += 1
                kn, tkn = knat[par], t_knat[par]
                sc.op("act", lambda e: e.activation(out=kn, in_=ps, func=AF.Copy), reads=[pt], writes=[tkn])
                if kdef[0] is not None:
                    kdef[0]()
                kdef[0] = lambda: k_tail(nch, h, par)
            self.linear_fm(wkv, D, 0, 768, hT2, [t_h2], evac_k, ntok=ST, banks=(0, 1))
            if kdef[0] is not None:
                kdef[0]()
                kdef[0] = None
            wv = wkv.rearrange("(kc p) n -> p kc n", p=128)
            vi = 0
            for g in range(3):
                d = GD[g]
                slab, stok = self.load_slab(wv[:, :, 768 + g * 256:768 + (g + 1) * 256], KC, 256)
                for r in range(d):
                    for jb in range(ST // (128 * d)):
                        b = vi % 4
                        par = vi % 4
                        vi += 1
                        t_lo = r + d * 128 * jb
                        cols = slice(t_lo, t_lo + d * 127 + 1, d)
                        for k in range(KC):
                            sc.op("pe", lambda e, b=b, k=k, cols=cols, slab=slab: e.matmul(
                                self.psf(b)[:, 0:256], hT2[:, k, cols], slab[:, k, :], start=(k == 0), stop=(k == KC - 1)),
                                reads=[t_h2, stok], writes=[self.ps_tok[b]])
                        if par % 2 == 0:
                            sc.op("act", lambda e, b=b, par=par: e.activation(out=vtmp[par], in_=self.psf(b)[:, 0:256], func=AF.Copy),
                                  reads=[self.ps_tok[b]], writes=[t_vtmp[par]])
                        else:
                            sc.op("dve", lambda e, b=b, par=par: e.tensor_copy(vtmp[par], self.psf(b)[:, 0:256]),
                                  reads=[self.ps_tok[b]], writes=[t_vtmp[par]])
                        row0 = r * (S // d) + st * (ST // d) + 128 * jb
                        sc.dma("sp", "vst%d" % par, lambda e, g=g, row0=row0, par=par: e.dma_start(
                            out=self.V_dram[g, row0:row0 + 128, :], in_=vtmp[par]), reads=[t_vtmp[par]], writes=[self.t_vdram2[par]])
        self.merge_deps(self.t_big0, allt)
        self.merge_deps(self.t_big1, allt)

    def attn_tile(self, li, ti):
        sc = self.sc
        S = self.S
        bj = li - self.n_a
        GD = (1, 4, 16)
        t0 = ti * T
        qperm = self.big(BF16, [12, T], 0)
        t_qp = Tok("qp")
        sqs = self.big(BF16, [KC, T], 0)
        off = 24576
        Kw, Vw, kwin_meta = [], [], []
        for g in range(3):
            d = GD[g]
            jlo = max(0, ti * T // d - 128) // 128
            jhi = ((ti + 1) * T // d - 1) // 128
            nb = jhi - jlo + 1
            kw = self.big(BF16, [2, d, nb * 128], off)
            off += 2 * d * nb * 128 * 2
            vw = self.big(BF16, [d, nb, 256], off)
            off += d * nb * 256 * 2
            Kw.append(kw)
            Vw.append(vw)
            kwin_meta.append((jlo, nb))
        cs = self.big(F32, [2, T], off)
        off += 8192
        qnat = [self.big(BF16, [HALF], off + i * 1024) for i in range(2)]
        off += 2048
        rt = self.tmpf
        PT = [self.big(BF16, [HALF], off + i * 1024) for i in range(2)]
        off += 2048
        acc_d = self.rstd
        ao = self.big(BF16, [4, T], off)
        off += 8192
        acc_o = self.big(F32, [T], off)
        off += 4096
        assert off <= self.big_n, off
        t_kw = [Tok("kw%d" % g) for g in range(3)]
        t_vw = [Tok("vw%d" % g) for g in range(3)]
        t_cs, t_rden = Tok("cs"), self.t_rstd
        t_qnat = [Tok("qn0"), Tok("qn1")]
        t_rt = self.t_tmpf
        t_PT = [Tok("PT0"), Tok("PT1")]
        t_ao = Tok("ao")
        t_acc = Tok("acc")
        allt = [t_qp, t_cs, t_ao, t_acc] + t_kw + t_vw + t_qnat + t_PT
        for t_ in allt:
            self.merge_deps(t_, [self.t_big0, self.t_big1])
        self.rmsnorm(li, self.hT, self.t_h, sqs, t_qp)
        sc.dma("sp", "csld", lambda e: e.dma_start(out=cs[0:32, 0, :], in_=self.d_cos[:, t0:t0 + T]), writes=[t_cs])
        sc.dma("sp", "csld", lambda e: e.dma_start(out=cs[0:32, 1, :], in_=self.d_sin[:, t0:t0 + T]), writes=[t_cs])
        for g in range(3):
            d = GD[g]
            jlo, nb = kwin_meta[g]
            for kvh in range(2):
                srck = self.K_dram[2 * g + kvh].rearrange("p (r j) -> p r j", r=d)[:, :, jlo * 128:(jlo + nb) * 128]
                sc.dma("sp", "kwl%d" % g, lambda e, g=g, kvh=kvh, srck=srck: e.dma_start(out=Kw[g][:, kvh, :, :], in_=srck),
                       reads=self.t_kdram2, writes=[t_kw[g]])
            srcv = self.V_dram[g].rearrange("(r jb kj) f -> kj r jb f", r=d, kj=128)
            if d == 1:
                sc.dma("sp", "vwl%d" % g, lambda e, g=g, srcv=srcv, jlo=jlo, nb=nb: e.dma_start(
                    out=Vw[g][:, 0, :, :], in_=srcv[:, 0, jlo:jlo + nb, :]), reads=self.t_vdram2, writes=[t_vw[g]])
            else:
                for jbi in range(nb):
                    sc.dma("sp", "vwl%d" % g, lambda e, g=g, srcv=srcv, jlo=jlo, jbi=jbi: e.dma_start(
                        out=Vw[g][:, :, jbi, :], in_=srcv[:, :, jlo + jbi, :]), reads=self.t_vdram2, writes=[t_vw[g]])
        wq = self.b_w_q[bj]
        scale = float(128 ** -0.5)
        qctr = [0]
        deferred = [None]
        for hh in range(2):
            for g in range(3):
                d = GD[g]

                def rope_tail(nch, h, par, g=g, d=d):
                    qn, tqn = qnat[par], t_qnat[par]
                    r_, tr_ = rt[par], t_rt[par]
                    sl = slice(h * HALF, (h + 1) * HALF)
                    b = 2 + par
                    sc.op("pe", lambda e: e.matmul(self.psf(b), self.protT, qn, start=True, stop=True),
                          reads=[tqn, self.t_const], writes=[self.ps_tok[b]])
                    sc.op("dve", lambda e: e.tensor_tensor(r_[0:32, :], self.psf(b)[0:32, :], cs[0:32, 1, sl], ALU.mult),
                          reads=[self.ps_tok[b], t_cs], writes=[tr_])
                    sc.op("dve", lambda e: e.tensor_tensor(qn[0:32, :], qn[0:32, :], cs[0:32, 0, sl], ALU.mult),
                          reads=[t_cs, tqn], writes=[tqn])
                    sc.op("dve", lambda e: e.tensor_tensor(qn[0:32, :], qn[0:32, :], r_[0:32, :], ALU.add),
                          reads=[tr_, tqn], writes=[tqn])
                    jn = HALF // d
                    dst = qperm[:, g * 4 + nch, :].rearrange("p (r j) -> p r j", r=d)[:, :, h * jn:(h + 1) * jn]
                    srcv = qn.rearrange("p (j r) -> p r j", r=d)
                    sc.op("act", lambda e: e.activation(out=dst, in_=srcv, func=AF.Copy), reads=[tqn], writes=[t_qp])

                def evac_q(nch, h, ps, pt, g=g, d=d, rope_tail=rope_tail):
                    par = qctr[0] % 2
                    qctr[0] += 1
                    qn, tqn = qnat[par], t_qnat[par]
                    sc.op("act", lambda e: e.activation(out=qn, in_=ps, func=AF.Copy), reads=[pt], writes=[tqn])
                    if deferred[0] is not None:
                        deferred[0]()
                    deferred[0] = lambda: rope_tail(nch, h, par)
                c0 = g * 1024 + hh * 512
                self.linear_fm(wq, D, c0, c0 + 512, self.hT, [self.t_h], evac_q)
            if deferred[0] is not None:
                deferred[0]()
                deferred[0] = None
            for hq4 in range(4):
                hq = hh * 4 + hq4
                kvh = hq // 4
                for g in range(3):
                    d = GD[g]
                    jlo, nb = kwin_meta[g]
                    nqb = T // d
                    i_lo = ti * nqb
                    i_hi = i_lo + nqb
                    qv = qperm[:, g * 4 + hq4, :].rearrange("p (r j) -> p r j", r=d)
                    st_ = {"col": 0, "items": [], "stash": None, "first": {4: True, 5: True, 6: True, 7: True}, "sb": 0}

                    def emit_pv(stash, g=g, st_=st_):
                        if stash is None:
                            return
                        pt_ap, tpt, items = stash
                        for (c_lo, nq, vblk, col0) in items:
                            q0 = 0
                            while q0 < nq:
                                c_ = col0 + q0
                                bank = 4 + c_ // HALF
                                n_ = min(nq - q0, HALF - c_ % HALF)
                                rhs = pt_ap[:, c_lo + q0:c_lo + q0 + n_]
                                osl = slice(c_ % HALF, c_ % HALF + n_)
                                f1 = st_["first"][bank]
                                st_["first"][bank] = False
                                sc.op("pe", lambda e, rhs=rhs, bank=bank, osl=osl, vblk=vblk, f1=f1: e.matmul(
                                    self.psf(bank)[:, osl], vblk, rhs, start=f1, stop=True, skip_group_check=True),
                                    reads=[tpt, t_vw[g]], writes=[self.ps_tok[bank]])
                                f2 = st_["first"][bank + 2]
                                st_["first"][bank + 2] = False
                                sc.op("pe", lambda e, rhs=rhs, bank=bank, osl=osl, f2=f2: e.matmul(
                                    self.psf(bank + 2)[:, osl], self.ones, rhs, start=f2, stop=True, skip_group_check=True),
                                    reads=[tpt, self.t_const], writes=[self.ps_tok[bank + 2]])
                                q0 += n_

                    def flush(final=False, st_=st_, emit_pv=emit_pv):
                        if st_["items"]:
                            b = 2 + st_["sb"] % 2
                            par = st_["sb"] % 2
                            st_["sb"] += 1
                            ncol = st_["col"]
                            pt_ap, tpt = PT[par], t_PT[par]
                            sc.op("act", lambda e: e.activation(out=pt_ap[:, 0:ncol], in_=self.psf(b)[:, 0:ncol], func=AF.Exp, scale=scale),
                                  reads=[self.ps_tok[b]], writes=[tpt])
                            emit_pv(st_["stash"])
                            st_["stash"] = (pt_ap, tpt, st_["items"])
                            st_["col"] = 0
                            st_["items"] = []
                            st_["masks"] = []
                        if final:
                            emit_pv(st_["stash"])
                            st_["stash"] = None
                    st_["masks"] = []
                    for r in range(d):
                        for jb in range(jlo, jlo + nb):
                            qs = max(i_lo, 128 * jb)
                            qe = min(i_hi, 128 * jb + 256)
                            if qe <= qs:
                                continue
                            nq = qe - qs
                            delta = qs - 128 * jb
                            if nq == 64:
                                mk_ = self.mask_ap[(delta, 64)]
                            elif delta == 128:
                                mk_ = self.mask_ap[(128, 128)]
                            else:
                                mk_ = self.mask_ap[(0, 256)][:, 0:nq]
                            if st_["col"] + nq > HALF:
                                flush()
                            b = 2 + st_["sb"] % 2
                            c_lo = st_["col"]
                            kblk = Kw[g][:, kvh, r, (jb - jlo) * 128:(jb - jlo + 1) * 128]
                            vblk = Vw[g][:, r, jb - jlo, kvh * 128:(kvh + 1) * 128]
                            qblk = qv[:, r, qs - i_lo:qe - i_lo]
                            out = self.psf(b)[:, c_lo:c_lo + nq]
                            sc.op("pe", lambda e, out=out, kblk=kblk, qblk=qblk: e.matmul(out, kblk, qblk, start=True, stop=False),
                                  reads=[t_kw[g], t_qp], writes=[self.ps_tok[b]])
                            sc.op("pe", lambda e, out=out, mk_=mk_: e.matmul(out, self.ident, mk_, start=False, stop=True),
                                  reads=[self.t_const], writes=[self.ps_tok[b]])
                            st_["items"].append((c_lo, nq, vblk, r * nqb + (qs - i_lo)))
                            st_["masks"].append((c_lo, nq, mk_))
                            st_["col"] += nq
                    flush(final=True)
                    for half in range(2):
                        sl = slice(half * HALF, (half + 1) * HALF)
                        if d == 1:
                            sc.op("act", lambda e, half=half, sl=sl: e.activation(out=acc_o[:, sl], in_=self.psf(4 + half), func=AF.Copy),
                                  reads=[self.ps_tok[4 + half]], writes=[t_acc])
                            sc.op("dve", lambda e, half=half, sl=sl: e.tensor_copy(acc_d[:, sl], self.psf(6 + half)),
                                  reads=[self.ps_tok[6 + half]], writes=[t_rden])
                        else:
                            rpb = HALF // nqb
                            for (acc, tk, bk) in ((acc_o, t_acc, 4 + half), (acc_d, t_rden, 6 + half)):
                                av_ = acc.rearrange("p (j r) -> p r j", r=d)[:, half * rpb:(half + 1) * rpb, :]
                                pv_ = self.psf(bk).rearrange("p (r j) -> p r j", r=rpb)
                                sc.op("dve", lambda e, av_=av_, pv_=pv_: e.tensor_tensor(av_, av_, pv_, ALU.add),
                                      reads=[self.ps_tok[bk], tk], writes=[tk])
                sc.op("act", lambda e: e.activation(out=acc_d, in_=acc_d, func=AF.Ln), reads=[t_rden], writes=[t_rden])
                sc.op("act", lambda e: e.activation(out=acc_d, in_=acc_d, func=AF.Exp, scale=-1.0), reads=[t_rden], writes=[t_rden])
                for half in range(2):
                    sl = slice(half * HALF, (half + 1) * HALF)
                    sc.op("dve", lambda e, sl=sl, hq4=hq4: e.tensor_tensor(ao[:, hq4, sl], acc_o[:, sl], acc_d[:, sl], ALU.mult),
                          reads=[t_acc, t_rden], writes=[t_ao])

            def evac_o(nch, h, ps, pt):
                sl = slice(h * HALF, (h + 1) * HALF)
                sc.op("dve", lambda e: e.tensor_tensor(self.xT[:, nch, sl], self.xT[:, nch, sl], ps, ALU.add),
                      reads=[pt, self.t_xk[nch]], writes=[self.t_xk[nch]])
            self.linear_fm(self.b_w_o[bj][hh * 512:(hh + 1) * 512, :], 512, 0, D, ao, [t_ao], evac_o)

        self.merge_deps(self.t_big0, allt)
        self.merge_deps(self.t_big1, allt)

    def load_x(self, src, src_tok, ti):
        t0 = ti * T
        sv_ = src.rearrange("(kc p) s -> p kc s", p=128)
        for k in range(KC):
            self.sc.dma("sp", "xld%d" % k, lambda e, k=k: e.dma_start(out=self.xT[:, k, :], in_=sv_[:, k, t0:t0 + T]),
                        reads=[src_tok[k]] if src_tok is not None else [], writes=[self.t_xk[k]])

    def store_x(self, ti):
        t0 = ti * T
        dv_ = self.x_scr.rearrange("(kc p) s -> p kc s", p=128)
        for k in range(KC):
            self.sc.dma("sp", "xst%d" % k, lambda e, k=k: e.dma_start(out=dv_[:, k, t0:t0 + T], in_=self.xT[:, k, :]),
                        reads=[self.t_xk[k]], writes=[self.t_xscr[ti][k]])

    def final_out(self, ti):
        t0 = ti * T
        L = self.n_layers
        sq = self.big(BF16, [KC, T], 0)
        of = self.big(F32, [KC, T], 65536)
        self.rmsnorm(2 * L, of, self.t_big1, sq, self.t_big0)
        self.sc.dma("sp", "ost", lambda e: e.dma_start(
            out=self.outT.rearrange("(kc p) s -> p kc s", p=128)[:, :, t0:t0 + T], in_=of),
            reads=[self.t_big1], writes=[self.t_out])

    def build(self):
        self.declare()
        self.alloc()
        self.t_big0 = Tok("big0")
        self.t_big1 = Tok("big1")
        self.t_pT = Tok("pT")
        self.setup_consts()
        L = self.n_layers
        for li in range(L):
            if self.mixers and li == self.n_a:
                self.setup_attn_consts()
                if li == 0:
                    self.kv_phase(self.xT_in, None)
                else:
                    self.kv_phase(self.x_scr, self.t_xscr)
            for ti in range(self.NT):
                if li == 0:
                    self.load_x(self.xT_in, None, ti)
                else:
                    self.load_x(self.x_scr, self.t_xscr[ti], ti)
                if self.mixers and li < self.n_a:
                    if ti == 0:
                        self.gdn_layer_consts(li)
                    self.gdn_tile(li, ti)
                elif self.mixers:
                    self.attn_tile(li, ti)
                    self.mlp_ple(li, ti)
                else:
                    self.mlp_ple(li, ti)
                if li == L - 1:
                    self.final_out(ti)
                else:
                    self.store_x(ti)
        self.sc.wait_all("sp", [self.t_out])
        self.sc.emit()
        return self.nc


_CACHE = {}


def host_inputs(S, b, inputs, gen):
    m = {}
    m["xT"] = np.ascontiguousarray(inputs["x"][b].T)
    L = gen.n_layers
    m["pT"] = np.ascontiguousarray(np.transpose(inputs["p"][:L, b], (0, 2, 1)))
    m["final_norm"] = np.ascontiguousarray(inputs["final_norm"])
    for k in ["attn_norm", "mlp_norm", "mlp_w_up", "mlp_w_down", "ple_w_proj", "ple_w_gate"]:
        m[k] = np.ascontiguousarray(inputs[k][:L])
    na = max(1, gen.n_a)
    m["a_w_in"] = np.ascontiguousarray(inputs["a_w_in"][:na])
    m["a_w_out"] = np.ascontiguousarray(inputs["a_w_out"][:na])
    cw = inputs["a_conv_w"][:na]
    m["convw"] = np.ascontiguousarray(cw.reshape(na, 4, 24, 128).transpose(0, 3, 1, 2))
    m["alog_rep"] = np.ascontiguousarray(np.broadcast_to(np.tile(inputs["a_log"][:na], (1, 8))[:, None, :], (na, 128, 64)))
    m["dtb_rep"] = np.ascontiguousarray(np.broadcast_to(np.tile(inputs["a_dt_bias"][:na], (1, 8))[:, None, :], (na, 128, 64)))
    m["onw"] = np.ascontiguousarray(inputs["a_out_norm"][:na][:, :, None])
    nbl = max(1, L - gen.n_a)
    m["b_w_kv"] = np.ascontiguousarray(inputs["b_w_kv"])
    m["b_w_q"] = np.ascontiguousarray(inputs["b_w_q"][:nbl])
    m["b_w_o"] = np.ascontiguousarray(inputs["b_w_o"][:nbl])
    m["kv_norm"] = np.ascontiguousarray(inputs["kv_norm"])
    for k, v in gen.consts_np.items():
        m[k] = v
    return m


def kernel(**inputs):
    inputs = {k: np.asarray(v) for k, v in inputs.items()}
    B, S, _ = inputs["x"].shape
    key = (S,)
    if key not in _CACHE:
        g = Gen(S)
        g.build()
        _CACHE[key] = g
    g = _CACHE[key]
    in_maps = [host_inputs(S, b, inputs, g) for b in range(B)]
    res = run_bass_kernel_spmd(g.nc, in_maps, core_ids=list(range(B)))
    out = np.stack([np.ascontiguousarray(r["outT"].T) for r in res.results], axis=0)
    return out.astype(np.float32)
```

```python
import numpy as np
import ml_dtypes
import concourse.bass as bass
import concourse.mybir as mybir
from concourse.bass_utils import run_bass_kernel_spmd

F32 = mybir.dt.float32
BF16 = mybir.dt.bfloat16
U8 = mybir.dt.uint8
AF = mybir.ActivationFunctionType
ALU = mybir.AluOpType

D = 1024
KC = 8
DFF = 4096
PLE = 256
EPS = 1e-6
T = 1024
HALF = 512
NEG = -30000.0


class Tok:
    __slots__ = ("w", "r", "name", "excl")

    def __init__(self, name="", excl=False):
        self.w = None
        self.r = {}
        self.name = name
        self.excl = excl


class Sched:
    ENGS = ["pe", "act", "dve", "pool", "sp"]

    def __init__(self, nc):
        self.nc = nc
        self.q = {e: [] for e in self.ENGS}
        self.cnt = {e: 0 for e in self.ENGS}
        self.waited = {e: {} for e in self.ENGS}
        self.dmacnt = {}

    def _deps(self, eng, reads, writes):
        deps = {}

        def add(k, v):
            if v > deps.get(k, 0):
                deps[k] = v
        for t in reads:
            if t.w is not None:
                add(*t.w)
        for t in writes:
            if t.w is not None:
                add(*t.w)
            for k, v in t.r.items():
                add(k, v)
        waits = []
        for k, v in deps.items():
            if k == eng and eng == "pe":
                continue
            if self.waited[eng].get(k, 0) < v:
                self.waited[eng][k] = v
                waits.append((k, v))
        return waits

    def op(self, eng, fn, reads=(), writes=()):
        ex = [t for t in reads if t.excl]
        if ex:
            reads = [t for t in reads if not t.excl]
            writes = list(writes) + ex
        waits = self._deps(eng, reads, writes)
        self.cnt[eng] += 1
        n = self.cnt[eng]
        self.q[eng].append((waits, fn, (eng, 1)))
        for t in reads:
            t.r[eng] = n
        for t in writes:
            t.w = (eng, n)
            t.r = {}

    def dma(self, queue, semkey, fn, reads=(), writes=()):
        waits = self._deps(queue, reads, writes)
        self.dmacnt[semkey] = self.dmacnt.get(semkey, 0) + 16
        n = self.dmacnt[semkey]
        self.q[queue].append((waits, fn, (semkey, 16)))
        for t in reads:
            t.r[semkey] = n
        for t in writes:
            t.w = (semkey, n)
            t.r = {}

    def wait_all(self, eng, toks):
        waits = self._deps(eng, toks, ())
        self.q[eng].append((waits, None, None))

    def emit(self):
        nc = self.nc
        keys = list(self.ENGS) + list(self.dmacnt.keys())
        import contextlib
        with contextlib.ExitStack() as es:
            sems = {k: es.enter_context(nc.semaphore("s_" + k)) for k in keys}
            block = es.enter_context(nc.Block())

            def replay(eng, e):
                for waits, fn, inc in self.q[eng]:
                    for k, v in waits:
                        e.wait_ge(sems[k], v)
                    if fn is not None:
                        ins = fn(e)
                        ins.then_inc(sems[inc[0]], inc[1])

            @block.tensor
            def _(e):
                replay("pe", e)

            @block.scalar
            def _(e):
                replay("act", e)

            @block.vector
            def _(e):
                replay("dve", e)

            @block.gpsimd
            def _(e):
                replay("pool", e)

            @block.sync
            def _(e):
                replay("sp", e)


class Arena:
    def __init__(self, nc, name, nbytes):
        self.t = nc.alloc_sbuf_tensor(name, [128, nbytes], U8)
        self.n = nbytes
        self.off = 0

    def take(self, dtype, shape, at=None):
        esz = 2 if dtype == BF16 else 4
        n = esz
        for s in shape:
            n *= s
        if at is None:
            at = self.off
            self.off += (n + 31) // 32 * 32
            assert self.off <= self.n, ("arena overflow", self.off, self.n)
        assert at + n <= self.n
        ap = self.t[:, at:at + n].bitcast(dtype)
        if len(shape) == 2:
            ap = ap.rearrange("p (a b) -> p a b", a=shape[0])
        elif len(shape) == 3:
            ap = ap.rearrange("p (a b c) -> p a b c", a=shape[0], b=shape[1])
        return ap


class Gen:
    def __init__(self, S, n_layers=4, n_a=2, mixers=True, stage=99):
        self.S = S
        self.stage = stage
        self.NT = S // T
        self.n_layers = n_layers
        self.n_a = n_a
        self.mixers = mixers
        self.nc = bass.Bass("TRN2", target_bir_lowering=False)
        self.sc = Sched(self.nc)
        self.consts_np = {}

    def dram_in(self, name, shape, dtype=F32):
        return self.nc.dram_tensor(name, list(shape), dtype, kind="ExternalInput").ap()

    def declare(self):
        nc, S = self.nc, self.S
        L = self.n_layers
        self.xT_in = self.dram_in("xT", [D, S])
        self.pT_in = self.dram_in("pT", [L, PLE, S])
        self.attn_norm = self.dram_in("attn_norm", [L, D])
        self.mlp_norm = self.dram_in("mlp_norm", [L, D])
        self.final_norm = self.dram_in("final_norm", [D])
        self.w_up = self.dram_in("mlp_w_up", [L, D, DFF])
        self.w_down = self.dram_in("mlp_w_down", [L, DFF, D])
        self.w_pproj = self.dram_in("ple_w_proj", [L, PLE, D])
        self.w_pgate = self.dram_in("ple_w_gate", [L, D, D])
        na = max(1, self.n_a)
        self.a_w_in = self.dram_in("a_w_in", [na, D, 4112])
        self.a_w_out = self.dram_in("a_w_out", [na, D, D])
        self.d_convw = self.dram_in("convw", [na, 128, 4, 24])
        self.d_alog = self.dram_in("alog_rep", [na, 128, 64])
        self.d_dtb = self.dram_in("dtb_rep", [na, 128, 64])
        self.d_onw = self.dram_in("onw", [na, 128, 1])
        self.t_dbg = Tok("dbg")
        nb = max(1, self.n_layers - self.n_a)
        self.b_w_kv = self.dram_in("b_w_kv", [D, 1536])
        self.b_w_q = self.dram_in("b_w_q", [nb, D, 3072])
        self.b_w_o = self.dram_in("b_w_o", [nb, D, D])
        self.kv_norm = self.dram_in("kv_norm", [D])
        self.K_dram = nc.dram_tensor("K_scr", [6, 128, S], BF16, kind="Internal").ap()
        self.V_dram = nc.dram_tensor("V_scr", [3, S, 256], BF16, kind="Internal").ap()
        self.t_kdram2 = [Tok("kdram0"), Tok("kdram1")]
        self.t_vdram2 = [Tok("vdram%d" % i) for i in range(4)]
        self.outT = nc.dram_tensor("outT", [D, S], F32, kind="ExternalOutput").ap()
        self.x_scr = nc.dram_tensor("x_scr", [D, S], F32, kind="Internal").ap()

    def alloc(self):
        nc = self.nc
        self.ar = Arena(nc, "arena", 207 * 1024)
        ar = self.ar
        self.ident = ar.take(BF16, [128])
        self.ones = ar.take(BF16, [128])
        self.normw = ar.take(F32, [2 * self.n_layers + 2, KC])
        self.tri_bf = ar.take(BF16, [128])
        self.maskS_bf = ar.take(BF16, [128])
        self.C1_bf = ar.take(BF16, [128])
        self.C2_bf = ar.take(BF16, [128])
        self.scalb = ar.take(BF16, [6, 64])
        self.convw = ar.take(F32, [4, 24])
        self.alog = ar.take(F32, [64])
        self.dtb = ar.take(F32, [64])
        self.nexpA = ar.take(F32, [64])
        self.onw = ar.take(F32, [1])
        self.halo = ar.take(BF16, [24, 3])
        self.st_off = ar.off
        self.Sst = ar.take(F32, [8, 128])
        self.Sbf = ar.take(BF16, [8, 128])
        self.scal = ar.take(F32, [17, 64])
        self.protT = ar.take(BF16, [128], at=self.st_off)
        self.maskbuf = ar.take(BF16, [640], at=self.st_off + 256)
        self.t_scal = Tok("scal")
        self.t_scalb = Tok("scalb")
        self.t_S = Tok("S")
        self.t_Sbf = Tok("Sbf")
        self.t_Sg = [Tok("S0"), Tok("S1")]
        self.t_Sbfg = [Tok("Sbf0"), Tok("Sbf1")]
        self.t_halo = Tok("halo")
        self.t_lconst = Tok("lconst")
        self.xT = ar.take(F32, [KC, T])
        self.hT = ar.take(BF16, [KC, T])
        self.rstd = ar.take(F32, [T])
        self.tmpf = [ar.take(F32, [HALF]) for _ in range(2)]
        self.NSLAB = 3
        self.slabs = [ar.take(BF16, [4096]) for _ in range(self.NSLAB)]
        self.slab_tok = [Tok("slab%d" % i) for i in range(self.NSLAB)]
        self.slab_i = 0
        self.big_off = ar.off
        self.big_n = ar.n - ar.off - 8192
        self.pT = ar.take(BF16, [2, T], at=ar.n - 8192)
        self.pwp = ar.take(BF16, [2, D], at=ar.n - 4096)
        self.ps = [nc.alloc_psum_tensor("ps%d" % i, [128, 2048], U8) for i in range(8)]
        self.ps_tok = [Tok("ps%d" % i, excl=True) for i in range(8)]
        self.t_xk = [Tok("x%d" % k) for k in range(KC)]
        self.t_h = Tok("h")
        self.t_rstd = Tok("rstd")
        self.t_tmpf = [Tok("tmpf%d" % i) for i in range(2)]
        self.t_const = Tok("const")
        self.t_xscr = [[Tok("xscr%d_%d" % (i, k)) for k in range(KC)] for i in range(self.NT)]
        self.t_out = Tok("out")

    def psf(self, b):
        return self.ps[b][:, :].bitcast(F32)

    def psb(self, b):
        return self.ps[b][:, :].bitcast(BF16)

    def big(self, dtype, shape, at):
        return self.ar.take(dtype, shape, at=self.big_off + at)

    def add_const(self, name, arr):
        arr = np.ascontiguousarray(arr)
        self.consts_np[name] = arr
        dt = BF16 if arr.dtype == ml_dtypes.bfloat16 else F32
        return self.dram_in(name, arr.shape, dt)

    def setup_consts(self):
        sc = self.sc
        ident = np.eye(128, dtype=np.float32).astype(ml_dtypes.bfloat16)
        ones = np.ones((128, 128), dtype=np.float32).astype(ml_dtypes.bfloat16)
        d_ident = self.add_const("c_ident", ident)
        d_ones = self.add_const("c_ones", ones)
        sc.dma("sp", "cst", lambda e: e.dma_start(out=self.ident, in_=d_ident), writes=[self.t_const])
        sc.dma("sp", "cst", lambda e: e.dma_start(out=self.ones, in_=d_ones), writes=[self.t_const])
        ar_ = np.arange(128)
        BIGC = 100.0
        fcs = {
            "c_tri": (ar_[:, None] <= ar_[None, :]).astype(np.float32).astype(ml_dtypes.bfloat16),
            "c_maskS": (ar_[:, None] > ar_[None, :]).astype(np.float32).astype(ml_dtypes.bfloat16),
            "c_C1": (BIGC * (ar_[:, None] > ar_[None, :])).astype(np.float32).astype(ml_dtypes.bfloat16),
            "c_C2": (BIGC * (ar_[:, None] <= ar_[None, :])).astype(np.float32).astype(ml_dtypes.bfloat16),
        }
        for nm, dst in [("c_tri", self.tri_bf), ("c_maskS", self.maskS_bf), ("c_C1", self.C1_bf), ("c_C2", self.C2_bf)]:
            d_ = self.add_const(nm, fcs[nm])
            sc.dma("sp", "cst", lambda e, d_=d_, dst=dst: e.dma_start(out=dst, in_=d_), writes=[self.t_const])
        L = self.n_layers
        for i in range(L):
            sc.dma("sp", "cst", lambda e, i=i: e.dma_start(
                out=self.normw[:, i, :], in_=self.attn_norm[i].rearrange("(kc p) -> p kc", p=128),
                allow_slow_non_contiguous=True), writes=[self.t_const])
            sc.dma("sp", "cst", lambda e, i=i: e.dma_start(
                out=self.normw[:, L + i, :], in_=self.mlp_norm[i].rearrange("(kc p) -> p kc", p=128),
                allow_slow_non_contiguous=True), writes=[self.t_const])
        sc.dma("sp", "cst", lambda e: e.dma_start(
            out=self.normw[:, 2 * L, :], in_=self.final_norm.rearrange("(kc p) -> p kc", p=128),
            allow_slow_non_contiguous=True), writes=[self.t_const])
        sc.dma("sp", "cst", lambda e: e.dma_start(
            out=self.normw[:, 2 * L + 1, :], in_=self.kv_norm.rearrange("(kc p) -> p kc", p=128),
            allow_slow_non_contiguous=True), writes=[self.t_const])
        sc.op("dve", lambda e: e.tensor_scalar(self.normw, self.normw, float(np.sqrt(D)), None, ALU.mult),
              reads=[self.t_const], writes=[self.t_const])

    def load_slab(self, src_ap, kc, nw):
        i = self.slab_i % self.NSLAB
        self.slab_i += 1
        assert kc * nw <= 4096
        dst = self.slabs[i][:, 0:kc * nw].rearrange("p (a b) -> p a b", a=kc)
        tok = self.slab_tok[i]
        self.sc.dma("pool", "slab%d" % i, lambda e: e.dma_start(out=dst, in_=src_ap), writes=[tok])
        return dst, tok

    def linear_fm(self, *a, **kw):
        for _ in self.linear_fm_g(*a, **kw):
            pass

    def linear_fm_g(self, w2d, K, n0, n1, rhs, rhs_toks, evac, nw=None, banks=(0, 1), ntok=T, halves=None):
        sc = self.sc
        kc = K // 128
        if nw is None:
            nw = 4096 // kc
        if halves is None:
            halves = range(ntok // HALF)
        wv = w2d.rearrange("(kc p) n -> p kc n", p=128)
        bi = 0
        for s0 in range(n0, n1, nw):
            s1 = min(s0 + nw, n1)
            slab, stok = self.load_slab(wv[:, :, s0:s1], kc, s1 - s0)
            for c0 in range(0, s1 - s0, 128):
                cw = min(128, s1 - s0 - c0)
                for h in halves:
                    b = banks[bi % len(banks)]
                    bi += 1
                    pt = self.ps_tok[b]
                    out = self.psf(b)[0:cw, :]
                    for k in range(kc):
                        sc.op("pe", lambda e, out=out, slab=slab, k=k, c0=c0, cw=cw, h=h: e.matmul(
                            out, slab[:, k, c0:c0 + cw], rhs[:, k, h * HALF:(h + 1) * HALF],
                            start=(k == 0), stop=(k == kc - 1)),
                            reads=[stok] + list(rhs_toks), writes=[pt])
                        if k % 8 == 7 and k != kc - 1:
                            yield
                    evac((s0 + c0 - n0) // 128, h, out, pt)
                    yield

    def rmsnorm(self, widx, out_ap, out_tok, sq_ap, sq_tok, halves=(0, 1), banks=(2, 3)):
        sc = self.sc
        for h in halves:
            sl = slice(h * HALF, (h + 1) * HALF)
            for k in range(KC):
                sc.op("act", lambda e, k=k, sl=sl: e.activation(out=sq_ap[:, k, sl], in_=self.xT[:, k, sl], func=AF.Square),
                      reads=[self.t_xk[k]], writes=[sq_tok])
        for i_, h in enumerate(halves):
            b = banks[i_ % len(banks)]
            pt = self.ps_tok[b]
            out = self.psf(b)
            for k in range(KC):
                sc.op("pe", lambda e, out=out, k=k, h=h: e.matmul(
                    out, self.ones, sq_ap[:, k, h * HALF:(h + 1) * HALF], start=(k == 0), stop=(k == KC - 1)),
                    reads=[sq_tok, self.t_const], writes=[pt])
            tf = self.tmpf[h]
            sc.op("act", lambda e, out=out, tf=tf: e.activation(out=tf, in_=out, func=AF.Ln, scale=1.0, bias=float(D * EPS)),
                  reads=[pt], writes=[self.t_tmpf[h]])
            sc.op("act", lambda e, tf=tf, h=h: e.activation(out=self.rstd[:, h * HALF:(h + 1) * HALF], in_=tf, func=AF.Exp, scale=-0.5),
                  reads=[self.t_tmpf[h]], writes=[self.t_rstd])
        for h in halves:
            for k in range(KC):
                sl = slice(h * HALF, (h + 1) * HALF)
                sc.op("dve", lambda e, k=k, sl=sl: e.scalar_tensor_tensor(
                    out_ap[:, k, sl], self.xT[:, k, sl], self.normw[:, widx, k:k + 1], self.rstd[:, sl],
                    ALU.mult, ALU.mult),
                    reads=[self.t_xk[k], self.t_rstd, self.t_const], writes=[out_tok])

    def mlp_ple(self, li, ti):
        for _ in self.mlp_ple_g(li, ti):
            pass

    def mlp_ple_g(self, li, ti, halves=(0, 1), lin_banks=(0, 1), aux_banks=(2, 3), t_up=None, prefetch=True):
        sc = self.sc
        L = self.n_layers
        t0 = ti * T
        upT = self.big(BF16, [32, T], 0)
        if t_up is None:
            t_up = self.t_big0
        if prefetch:
            self.ple_prefetch(li, ti)
        pT, pwp, t_pT = self.pT, self.pwp, self.t_pT
        sq = self.big(BF16, [KC, T], 0)
        self.rmsnorm(L + li, self.hT, self.t_h, sq, t_up, halves=halves, banks=aux_banks)
        yield
        ei = [0]
        wide_banks = (0, 1, 4, 5, 6, 7) if tuple(lin_banks) == (0, 1) else lin_banks

        def evac_up(nch, h, ps, pt):
            j = ei[0] % 2
            ei[0] += 1
            tf = self.tmpf[j]
            sc.op("act", lambda e: e.activation(out=tf, in_=ps, func=AF.Relu), reads=[pt], writes=[self.t_tmpf[j]])
            sc.op("dve", lambda e: e.tensor_tensor(upT[:, nch, h * HALF:(h + 1) * HALF], tf, tf, ALU.mult),
                  reads=[self.t_tmpf[j]], writes=[t_up])
        yield from self.linear_fm_g(self.w_up[li], D, 0, DFF, self.hT, [self.t_h], evac_up, banks=wide_banks, halves=halves)
        xb = self.hT

        def evac_down(nch, h, ps, pt):
            sl = slice(h * HALF, (h + 1) * HALF)
            sc.op("dve", lambda e: e.tensor_tensor(self.xT[:, nch, sl], self.xT[:, nch, sl], ps, ALU.add),
                  reads=[pt, self.t_xk[nch]], writes=[self.t_xk[nch]])
            sc.op("act", lambda e: e.activation(out=xb[:, nch, sl], in_=self.xT[:, nch, sl], func=AF.Copy),
                  reads=[self.t_xk[nch]], writes=[self.t_h])
        yield from self.linear_fm_g(self.w_down[li], DFF, 0, D, upT, [t_up], evac_down, banks=wide_banks, halves=halves)
        gate_banks = lin_banks[:1] if len(aux_banks) == 0 or aux_banks[0] in lin_banks else lin_banks
        pp_banks = [b for b in aux_banks if b not in gate_banks] or [lin_banks[-1]]
        pctr = [0]

        def evac_gate(nch, h, ps, pt):
            j = ei[0] % 2
            ei[0] += 1
            tf = self.tmpf[j]
            sl = slice(h * HALF, (h + 1) * HALF)
            bp = pp_banks[pctr[0] % len(pp_banks)]
            pctr[0] += 1
            for k in range(2):
                sc.op("pe", lambda e, k=k, bp=bp: e.matmul(self.psf(bp), pwp[:, k, nch * 128:(nch + 1) * 128], pT[:, k, sl],
                                                           start=(k == 0), stop=(k == 1)),
                      reads=[t_pT], writes=[self.ps_tok[bp]])
            sc.op("act", lambda e: e.activation(out=tf, in_=ps, func=AF.Sigmoid), reads=[pt], writes=[self.t_tmpf[j]])
            sc.op("dve", lambda e, bp=bp: e.tensor_tensor(tf, tf, self.psf(bp), ALU.mult),
                  reads=[self.t_tmpf[j], self.ps_tok[bp]], writes=[self.t_tmpf[j]])
            sc.op("dve", lambda e: e.tensor_tensor(self.xT[:, nch, sl], self.xT[:, nch, sl], tf, ALU.add),
                  reads=[self.t_tmpf[j], self.t_xk[nch]], writes=[self.t_xk[nch]])
        yield from self.linear_fm_g(self.w_pgate[li], D, 0, D, xb, [self.t_h], evac_gate, banks=gate_banks, halves=halves)

    def ple_prefetch(self, li, ti):
        t0 = ti * T
        self.sc.dma("pool", "pld", lambda e: e.dma_start(
            out=self.pT, in_=self.pT_in[li].rearrange("(kc p) s -> p kc s", p=128)[:, :, t0:t0 + T]), writes=[self.t_pT])
        self.sc.dma("pool", "pwld", lambda e: e.dma_start(
            out=self.pwp, in_=self.w_pproj[li].rearrange("(kc p) n -> p kc n", p=128)), writes=[self.t_pT])

    def merge_deps(self, dst, srcs):
        for s_ in srcs:
            items = list(s_.r.items())
            if s_.w is not None:
                items.append(s_.w)
            for k, v in items:
                if v > dst.r.get(k, 0):
                    dst.r[k] = v

    def dbg(self, name, ap, tok, shape, dtype=F32):
        d = self.nc.dram_tensor("dbg_" + name, list(shape), dtype, kind="ExternalOutput").ap()
        self.sc.dma("sp", "dbg", lambda e: e.dma_start(out=d, in_=ap), reads=[tok], writes=[self.t_dbg])

    def gdn_layer_consts(self, li):
        sc = self.sc
        w = [self.t_lconst]
        sc.dma("sp", "lcst", lambda e: e.dma_start(out=self.convw, in_=self.d_convw[li]), writes=w)
        sc.dma("sp", "lcst", lambda e: e.dma_start(out=self.alog, in_=self.d_alog[li]), writes=w)
        sc.dma("sp", "lcst", lambda e: e.dma_start(out=self.dtb, in_=self.d_dtb[li]), writes=w)
        sc.dma("sp", "lcst", lambda e: e.dma_start(out=self.onw, in_=self.d_onw[li]), writes=w)
        sc.op("act", lambda e: e.activation(out=self.nexpA, in_=self.alog, func=AF.Exp), reads=w, writes=w)
        sc.op("dve", lambda e: e.tensor_scalar(self.nexpA, self.nexpA, -1.0, None, ALU.mult), reads=w, writes=w)
        sc.op("dve", lambda e: e.memset(self.halo, 0.0), writes=[self.t_halo])
        sc.op("dve", lambda e: e.memset(self.Sst, 0.0), writes=[self.t_S] + self.t_Sg)
        sc.op("dve", lambda e: e.memset(self.Sbf, 0.0), writes=[self.t_Sbf] + self.t_Sbfg)

    def gdn_tile(self, li, ti):
        sc = self.sc
        ai = li
        BIGC = 100.0
        qT = self.big(BF16, [KC, T], 0)
        kT = self.big(BF16, [KC, T], 16384)
        vT = self.big(BF16, [KC, T], 32768)
        zs = self.big(BF16, [KC, T], 49152)
        W0 = 65536
        sq = self.big(BF16, [KC, T], W0)
        qsq = self.big(BF16, [KC, T], W0)
        ksq = self.big(BF16, [KC, T], W0 + 16384)
        t_q, t_k, t_v, t_z, t_sq = Tok("q"), Tok("k"), Tok("v"), Tok("z"), Tok("sqw")
        self.merge_deps(t_q, [self.t_big0, self.t_big1])
        self.merge_deps(t_k, [self.t_big0, self.t_big1])
        self.merge_deps(t_v, [self.t_big0, self.t_big1])
        self.merge_deps(t_z, [self.t_big0, self.t_big1])
        self.merge_deps(t_sq, [self.t_big0, self.t_big1])
        CB = W0 + 20544
        pc = [self.big(BF16, [T + 8], CB + i * 2080) for i in range(2)]
        t_pc = [Tok("pc0"), Tok("pc1")]
        diag = [self.big(BF16, [4, 128], CB + 4160 + i * 1024) for i in range(2)]
        t_diag = [Tok("dg0"), Tok("dg1")]
        for t_ in t_pc + t_diag:
            self.merge_deps(t_, [self.t_big0, self.t_big1])

        self.ple_prefetch(li, ti)
        self.rmsnorm(li, self.hT, self.t_h, sq, t_sq)

        wv = self.a_w_in[ai].rearrange("(kc p) n -> p kc n", p=128)
        slab, stok = self.load_slab(wv[:, :, 4096:4112], KC, 16)
        ba = self.psf(4)[:, 0:128].rearrange("p (b c) -> p b c", b=8)
        for tb in range(8):
            for k in range(KC):
                sc.op("pe", lambda e, tb=tb, k=k: e.matmul(
                    self.psf(4)[:, tb * 16:(tb + 1) * 16], self.hT[:, k, tb * 128:(tb + 1) * 128], slab[:, k, :],
                    start=(k == 0), stop=(k == KC - 1)), reads=[stok, self.t_h], writes=[self.ps_tok[4]])
        SB, SG, SGC, SEG, SEGL, SCD, SLNK, SRK, SIRK, SSQ, S1, S2, S3, S4, ST0, ST1, ST2 = range(17)
        scal = self.scal
        ts = self.t_scal

        def sv(kind):
            return scal[:, kind, :]

        def sv3(kind):
            return scal[:, kind, :].rearrange("p (b c) -> p b c", b=8)
        p4 = self.ps_tok[4]
        sc.op("act", lambda e: e.activation(out=sv3(SB), in_=ba[:, :, 0:8], func=AF.Sigmoid), reads=[p4], writes=[ts])
        sc.op("dve", lambda e: e.tensor_tensor(sv3(ST0), ba[:, :, 8:16], self.dtb.rearrange("p (b c) -> p b c", b=8), ALU.add),
              reads=[p4, self.t_lconst], writes=[ts])
        sc.op("dve", lambda e: e.tensor_scalar(sv(ST2), sv(ST0), 0.0, None, ALU.max), reads=[ts], writes=[ts])
        sc.op("dve", lambda e: e.scalar_tensor_tensor(sv(ST1), sv(ST2), -2.0, sv(ST0), ALU.mult, ALU.add), reads=[ts], writes=[ts])
        sc.op("act", lambda e: e.activation(out=sv(ST1), in_=sv(ST1), func=AF.Exp), reads=[ts], writes=[ts])
        sc.op("act", lambda e: e.activation(out=sv(ST1), in_=sv(ST1), func=AF.Ln, bias=1.0), reads=[ts], writes=[ts])
        sc.op("dve", lambda e: e.tensor_tensor(sv(ST0), sv(ST2), sv(ST1), ALU.add), reads=[ts], writes=[ts])
        sc.op("dve", lambda e: e.tensor_tensor(sv(SG), sv(ST0), self.nexpA, ALU.mult), reads=[ts, self.t_lconst], writes=[ts])
        p5 = self.ps_tok[5]
        sb_ = self.scalb
        tsb = self.t_scalb
        GH, GL, L4H, L4L, LRH, LRL = range(6)
        sc.op("act", lambda e: e.activation(out=sb_[:, GH, :], in_=sv(SG), func=AF.Copy), reads=[ts], writes=[tsb])
        sc.op("dve", lambda e: e.tensor_tensor(sb_[:, GL, :], sv(SG), sb_[:, GH, :], ALU.subtract), reads=[ts, tsb], writes=[tsb])
        sc.op("pe", lambda e: e.matmul(self.psf(5)[:, 0:64], self.tri_bf, sb_[:, GH, :], start=True, stop=False),
              reads=[tsb, self.t_const], writes=[p5])
        sc.op("pe", lambda e: e.matmul(self.psf(5)[:, 0:64], self.tri_bf, sb_[:, GL, :], start=False, stop=True),
              reads=[tsb, self.t_const], writes=[p5])
        sc.op("pe", lambda e: e.matmul(self.psf(5)[:, 64:128], self.ones, sb_[:, GH, :], start=True, stop=False),
              reads=[tsb, self.t_const], writes=[p5])
        sc.op("pe", lambda e: e.matmul(self.psf(5)[:, 64:128], self.ones, sb_[:, GL, :], start=False, stop=True),
              reads=[tsb, self.t_const], writes=[p5])
        sc.op("act", lambda e: e.activation(out=sv(SGC), in_=self.psf(5)[:, 0:64], func=AF.Copy), reads=[p5], writes=[ts])
        sc.op("act", lambda e: e.activation(out=sv(SEG), in_=self.psf(5)[:, 0:64], func=AF.Exp), reads=[p5], writes=[ts])
        sc.op("act", lambda e: e.activation(out=sv(SCD), in_=self.psf(5)[:, 64:128], func=AF.Exp), reads=[p5], writes=[ts])
        sc.op("dve", lambda e: e.tensor_tensor(sv(SEGL), self.psf(5)[:, 64:128], sv(SGC), ALU.subtract), reads=[p5, ts], writes=[ts])
        sc.op("act", lambda e: e.activation(out=sv(SEGL), in_=sv(SEGL), func=AF.Exp), reads=[ts], writes=[ts])
        dests = [qT, kT, vT]
        dtoks = [t_q, t_k, t_v]
        cstate = {}

        cdef = [None]

        def conv_tail(nch, par):
            pcb, tp = pc[par], t_pc[par]
            dst = dests[nch // 8]
            dtok = dtoks[nch // 8]
            for hh in range(2):
                b = 2 + hh
                for j in range(4):
                    sc.op("pe", lambda e, b=b, j=j, hh=hh: e.matmul(
                        self.psf(b), diag[par][:, j, :], pcb[:, j + hh * HALF:j + (hh + 1) * HALF],
                        start=(j == 0), stop=(j == 3)), reads=[tp, t_diag[par]], writes=[self.ps_tok[b]])
                sc.op("act", lambda e, b=b, hh=hh: e.activation(
                    out=dst[:, nch % 8, hh * HALF:(hh + 1) * HALF], in_=self.psf(b), func=AF.Silu),
                    reads=[self.ps_tok[b]], writes=[dtok])

        def evac_qkv(nch, h, ps, pt):
            par = nch % 2
            pcb, tp = pc[par], t_pc[par]
            if h == 0:
                sc.op("dve", lambda e: e.tensor_copy(pcb[:, 0:3], self.halo[:, nch, :]), reads=[self.t_halo], writes=[tp])
                for j in range(4):
                    sc.op("dve", lambda e, j=j: e.tensor_scalar(diag[par][:, j, :], self.ident, self.convw[:, j, nch:nch + 1], None, ALU.mult),
                          reads=[self.t_const, self.t_lconst], writes=[t_diag[par]])
            sc.op("act", lambda e: e.activation(out=pcb[:, 3 + h * HALF:3 + (h + 1) * HALF], in_=ps, func=AF.Copy),
                  reads=[pt], writes=[tp])
            if h == 0 and cdef[0] is not None:
                cdef[0]()
                cdef[0] = None
            if h == 1:
                sc.op("dve", lambda e: e.tensor_copy(self.halo[:, nch, :], pcb[:, T:T + 3]), reads=[tp], writes=[self.t_halo])
                cdef[0] = lambda: conv_tail(nch, par)
        self.linear_fm(self.a_w_in[ai], D, 0, 3072, self.hT, [self.t_h], evac_qkv)
        if cdef[0] is not None:
            cdef[0]()
            cdef[0] = None

        def evac_z(nch, h, ps, pt):
            sc.op("act", lambda e: e.activation(out=zs[:, nch, h * HALF:(h + 1) * HALF], in_=ps, func=AF.Silu),
                  reads=[pt], writes=[t_z])
        self.linear_fm(self.a_w_in[ai], D, 3072, 4096, self.hT, [self.t_h], evac_z)

        if getattr(self, "stage", 99) < 2:
            return
        for k in range(KC):
            sc.op("act", lambda e, k=k: e.activation(out=qsq[:, k, :], in_=qT[:, k, :], func=AF.Square), reads=[t_q], writes=[t_sq])
            sc.op("dve", lambda e, k=k: e.tensor_tensor(ksq[:, k, :], kT[:, k, :], kT[:, k, :], ALU.mult), reads=[t_k], writes=[t_sq])
        p6 = self.ps_tok[6]
        for cb in range(8):
            for h in range(8):
                col = cb * 8 + h
                sc.op("pe", lambda e, cb=cb, h=h, col=col: e.matmul(
                    self.psf(6)[:, col:col + 1], qsq[:, h, cb * 128:(cb + 1) * 128], self.ones[:, 0:1], start=True, stop=True),
                    reads=[t_sq, self.t_const], writes=[p6])
                sc.op("pe", lambda e, cb=cb, h=h, col=col: e.matmul(
                    self.psf(6)[:, 64 + col:65 + col], ksq[:, h, cb * 128:(cb + 1) * 128], self.ones[:, 0:1], start=True, stop=True),
                    reads=[t_sq, self.t_const], writes=[p6])
        sc.op("act", lambda e: e.activation(out=sv(SLNK), in_=self.psf(6)[:, 64:128], func=AF.Ln, bias=EPS), reads=[p6], writes=[ts])
        sc.op("act", lambda e: e.activation(out=sv(SRK), in_=sv(SLNK), func=AF.Exp, scale=-0.5), reads=[ts], writes=[ts])
        sc.op("act", lambda e: e.activation(out=sv(SIRK), in_=sv(SLNK), func=AF.Exp, scale=0.5), reads=[ts], writes=[ts])
        sc.op("act", lambda e: e.activation(out=sv(SSQ), in_=self.psf(6)[:, 0:64], func=AF.Ln, bias=EPS), reads=[p6], writes=[ts])
        sc.op("act", lambda e: e.activation(out=sv(SSQ), in_=sv(SSQ), func=AF.Exp, scale=-0.5), reads=[ts], writes=[ts])
        sc.op("dve", lambda e: e.tensor_scalar(sv(SSQ), sv(SSQ), float(128 ** -0.5), None, ALU.mult), reads=[ts], writes=[ts])
        SL4, SLRK = S4, ST1
        sc.op("dve", lambda e: e.tensor_tensor(sv(S3), sv(SRK), sv(SB), ALU.mult), reads=[ts], writes=[ts])
        sc.op("dve", lambda e: e.tensor_tensor(sv(ST0), sv(S3), sv(SRK), ALU.mult), reads=[ts], writes=[ts])
        sc.op("dve", lambda e: e.tensor_tensor(sv(S1), sv(ST0), sv(SEG), ALU.mult), reads=[ts], writes=[ts])
        sc.op("dve", lambda e: e.tensor_tensor(sv(S2), sv(SRK), sv(SEGL), ALU.mult), reads=[ts], writes=[ts])
        sc.op("act", lambda e: e.activation(out=sv(ST2), in_=sv(SB), func=AF.Ln), reads=[ts], writes=[ts])
        sc.op("dve", lambda e: e.tensor_tensor(sv(SL4), sv(ST2), sv(SLNK), ALU.subtract), reads=[ts], writes=[ts])
        sc.op("dve", lambda e: e.tensor_scalar(sv(SL4), sv(SL4), -80.0, None, ALU.max), reads=[ts], writes=[ts])
        sc.op("dve", lambda e: e.tensor_scalar(sv(SLRK), sv(SLNK), -0.5, None, ALU.mult), reads=[ts], writes=[ts])
        sc.op("act", lambda e: e.activation(out=sb_[:, L4H, :], in_=sv(SL4), func=AF.Copy), reads=[ts], writes=[tsb])
        sc.op("dve", lambda e: e.tensor_tensor(sb_[:, L4L, :], sv(SL4), sb_[:, L4H, :], ALU.subtract), reads=[ts, tsb], writes=[tsb])
        sc.op("act", lambda e: e.activation(out=sb_[:, LRH, :], in_=sv(SLRK), func=AF.Copy), reads=[ts], writes=[tsb])
        sc.op("dve", lambda e: e.tensor_tensor(sb_[:, LRL, :], sv(SLRK), sb_[:, LRH, :], ALU.subtract), reads=[ts, tsb], writes=[tsb])

        if getattr(self, "stage", 99) < 3:
            return
        names = ["kbg", "kd", "vb", "Gm", "E", "Pa", "Pta", "Pb", "Ptb", "Rm", "attnT", "nwT", "vnew", "avsb", "opp", "onb", "osc"]
        names = names + ["GmL"]
        Ws, wts = [], []
        wo = [W0]

        def wb(dtype, shape):
            esz = 2 if dtype == BF16 else 4
            n = esz
            for s_ in shape:
                n *= s_
            ap = self.big(dtype, shape, wo[0])
            wo[0] += (n + 31) // 32 * 32
            assert wo[0] <= self.big_n, wo[0]
            return ap
        NHT = getattr(self, "gdn_nht", 2)
        NTH = 8 // NHT
        for hg_ in range(NTH):
            W = {}
            for n_ in ["kbg", "kd", "vb", "Pa", "Pta", "Pb", "Ptb", "Rm", "attnT", "nwT", "vnew", "onb", "Gm", "GmL"]:
                W[n_] = wb(BF16, [NHT, 128])
            for n_ in ["E", "avsb", "opp"]:
                W[n_] = wb(F32, [NHT, 128])
            W["osc"] = wb(F32, [2 * NHT])
            wt = {n_: Tok(n_ + str(hg_)) for n_ in names}
            for n_ in names:
                self.merge_deps(wt[n_], [t_sq] + t_pc + t_diag)
                wt[n_].r["pe"] = sc.cnt["pe"]
            Ws.append(W)
            wts.append(wt)
        og = self.hT
        t_ogs = [Tok("og%d" % i_) for i_ in range(NTH)]
        if len(self.t_Sg) != NTH:
            self.t_Sg = [Tok("S%d" % i_) for i_ in range(NTH)]
            self.t_Sbfg = [Tok("Sbf%d" % i_) for i_ in range(NTH)]
            for t_ in self.t_Sg:
                self.merge_deps(t_, [self.t_S])
                t_.w = self.t_S.w
            for t_ in self.t_Sbfg:
                self.merge_deps(t_, [self.t_Sbf])
                t_.w = self.t_Sbf.w
        for t_ in t_ogs:
            self.merge_deps(t_, [self.t_h])
        ps_t = self.ps_tok
        bank_ctr = [0] * NTH
        OVERLAP = getattr(self, "gdn_overlap", False)

        def body(cb, hg):
            W, wt = Ws[hg], wts[hg]
            kbg, kd, vb, Gm, E, Rm = W["kbg"], W["kd"], W["vb"], W["Gm"], W["E"], W["Rm"]
            GmL = W["GmL"]
            sb_ = self.scalb
            tsb = self.t_scalb

            def cb1(kind, hh):
                c = col(hh)
                return sb_[:, kind, c:c + 1]
            Pa, Pta, Pb, Ptb = W["Pa"], W["Pta"], W["Pb"], W["Ptb"]
            attnT, nwT, vnew, avsb, opp, onb, osc = W["attnT"], W["nwT"], W["vnew"], W["avsb"], W["opp"], W["onb"], W["osc"]
            t_S, t_Sbf, t_og = self.t_Sg[hg], self.t_Sbfg[hg], t_ogs[hg]
            tsl = slice(cb * 128, (cb + 1) * 128)

            SW = NHT * 128

            def nb():
                if NHT == 4:
                    nbk = 3 if OVERLAP else 4
                    s_ = bank_ctr[hg] % nbk
                    bank_ctr[hg] += 1
                    return (hg * nbk + s_, 0)
                if OVERLAP:
                    s_ = bank_ctr[hg] % 3
                    bank_ctr[hg] += 1
                    slot = hg * 3 + s_
                    return (slot // 2, (slot % 2) * SW)
                s_ = bank_ctr[hg] % 4
                bank_ctr[hg] += 1
                return (hg * 2 + s_ // 2, (s_ % 2) * SW)

            def pf(sl_, hh=None):
                b_, o_ = sl_
                if hh is None:
                    return self.psf(b_)[:, o_:o_ + SW]
                return self.psf(b_)[:, o_ + hh * 128:o_ + (hh + 1) * 128]

            def pb(sl_, hh=None, second=False):
                b_, o_ = sl_
                base = 2 * o_ + (SW if second else 0)
                if hh is None:
                    return self.psb(b_)[:, base:base + SW]
                return self.psb(b_)[:, base + hh * 128:base + (hh + 1) * 128]

            def pt_(sl_):
                return ps_t[sl_[0]]

            def col(hh):
                return cb * 8 + hg * NHT + hh

            def c1(kind, hh):
                c = col(hh)
                return scal[:, kind, c:c + 1]

            def f4(sl_):
                return pf(sl_).rearrange("p (a b) -> p a b", a=NHT)
            H = [hg * NHT + hh for hh in range(NHT)]
            R4 = range(NHT)
            b_tr = nb()
            for hh in R4:
                sc.op("pe", lambda e, hh=hh: e.transpose(pb(b_tr, hh), kT[:, H[hh], tsl], self.ident),
                      reads=[t_k, self.t_const], writes=[pt_(b_tr)])
                sc.op("pe", lambda e, hh=hh: e.transpose(pb(b_tr, hh, True), vT[:, H[hh], tsl], self.ident),
                      reads=[t_v, self.t_const], writes=[pt_(b_tr)])
            yield
            for hh in R4:
                sc.op("act", lambda e, hh=hh: e.activation(out=kbg[:, hh, :], in_=pb(b_tr, hh), func=AF.Copy, scale=c1(S1, hh)),
                      reads=[pt_(b_tr), ts], writes=[wt["kbg"]])
                sc.op("dve", lambda e, hh=hh: e.tensor_scalar(kd[:, hh, :], pb(b_tr, hh), c1(S2, hh), None, ALU.mult),
                      reads=[pt_(b_tr), ts], writes=[wt["kd"]])
                sc.op("act", lambda e, hh=hh: e.activation(out=vb[:, hh, :], in_=pb(b_tr, hh, True), func=AF.Copy, scale=c1(S3, hh)),
                      reads=[pt_(b_tr), ts], writes=[wt["vb"]])
            b_kk = nb()
            b_d = nb()
            for hh in R4:
                sc.op("pe", lambda e, hh=hh: e.matmul(pf(b_kk, hh), kT[:, H[hh], tsl], kT[:, H[hh], tsl], start=True, stop=True),
                      reads=[t_k], writes=[pt_(b_kk)])
                sc.op("dve", lambda e, hh=hh: e.tensor_scalar(Gm[:, hh, :], self.maskS_bf, cb1(0, hh), None, ALU.mult),
                      reads=[tsb, self.t_const], writes=[wt["Gm"]])
                sc.op("dve", lambda e, hh=hh: e.tensor_scalar(GmL[:, hh, :], self.maskS_bf, cb1(1, hh), None, ALU.mult),
                      reads=[tsb, self.t_const], writes=[wt["GmL"]])
            for hh in R4:
                o_ = pf(b_d, hh)
                sc.op("pe", lambda e, hh=hh, o_=o_: e.matmul(o_, self.tri_bf, Gm[:, hh, :], start=True, stop=False),
                      reads=[wt["Gm"], self.t_const], writes=[pt_(b_d)])
                sc.op("pe", lambda e, hh=hh, o_=o_: e.matmul(o_, self.tri_bf, GmL[:, hh, :], start=False, stop=False),
                      reads=[wt["GmL"], self.t_const], writes=[pt_(b_d)])
                sc.op("pe", lambda e, hh=hh, o_=o_: e.matmul(o_, self.ident, cb1(2, hh).to_broadcast([128, 128]), start=False, stop=False),
                      reads=[tsb, self.t_const], writes=[pt_(b_d)])
                sc.op("pe", lambda e, hh=hh, o_=o_: e.matmul(o_, self.ident, cb1(3, hh).to_broadcast([128, 128]), start=False, stop=False),
                      reads=[tsb, self.t_const], writes=[pt_(b_d)])
                sc.op("pe", lambda e, hh=hh, o_=o_: e.matmul(o_, self.ident, self.C1_bf, start=False, stop=True),
                      reads=[self.t_const], writes=[pt_(b_d)])
            yield
            sc.op("act", lambda e: e.activation(out=E, in_=f4(b_d), func=AF.Exp, bias=-BIGC), reads=[pt_(b_d)], writes=[wt["E"]])
            sc.op("dve", lambda e: e.tensor_tensor(Pta, f4(b_kk), E, ALU.mult), reads=[pt_(b_kk), wt["E"]], writes=[wt["Pta"]])
            yield
            b_p = nb()
            for hh in R4:
                sc.op("pe", lambda e, hh=hh: e.transpose(pb(b_p, hh), Pta[:, hh, :], self.ident),
                      reads=[wt["Pta"], self.t_const], writes=[pt_(b_p)])
            yield
            sc.op("act", lambda e: e.activation(out=Pa, in_=pb(b_p).rearrange("p (a b) -> p a b", a=NHT), func=AF.Copy),
                  reads=[pt_(b_p)], writes=[wt["Pa"]])
            sc.op("dve", lambda e: e.tensor_tensor(Rm, self.ident.unsqueeze(1).to_broadcast([128, NHT, 128]),
                                                   pb(b_p).rearrange("p (a b) -> p a b", a=NHT), ALU.subtract),
                  reads=[pt_(b_p), self.t_const], writes=[wt["Rm"]])
            cur = (Pa, Pta, "Pa", "Pta")
            nxt = (Pb, Ptb, "Pb", "Ptb")
            for lvl in range(6):
                P_, Pt_, nP, nPt = cur
                Pn, Ptn, nPn, nPtn = nxt
                last = (lvl == 5)
                b_qt = nb()
                for hh in R4:
                    sc.op("pe", lambda e, hh=hh, P_=P_, Pt_=Pt_, b_qt=b_qt: e.matmul(pf(b_qt, hh), P_[:, hh, :], Pt_[:, hh, :], start=True, stop=True),
                          reads=[wt[nP], wt[nPt]], writes=[pt_(b_qt)])
                if not last:
                    b_q = nb()
                    for hh in R4:
                        sc.op("pe", lambda e, hh=hh, P_=P_, Pt_=Pt_, b_q=b_q: e.matmul(pf(b_q, hh), Pt_[:, hh, :], P_[:, hh, :], start=True, stop=True),
                              reads=[wt[nP], wt[nPt]], writes=[pt_(b_q)])
                yield
                sc.op("act", lambda e, Ptn=Ptn, b_qt=b_qt: e.activation(out=Ptn, in_=f4(b_qt), func=AF.Copy),
                      reads=[pt_(b_qt)], writes=[wt[nPtn]])
                if not last:
                    sc.op("dve", lambda e, Pn=Pn, b_q=b_q: e.tensor_copy(Pn, f4(b_q)), reads=[pt_(b_q)], writes=[wt[nPn]])
                b_r = nb()
                for hh in R4:
                    sc.op("pe", lambda e, hh=hh, Ptn=Ptn, b_r=b_r: e.matmul(pf(b_r, hh), Ptn[:, hh, :], Rm[:, hh, :], start=True, stop=True),
                          reads=[wt[nPtn], wt["Rm"]], writes=[pt_(b_r)])
                yield
                sc.op("dve", lambda e, b_r=b_r: e.tensor_tensor(Rm, Rm, f4(b_r), ALU.add),
                      reads=[pt_(b_r), wt["Rm"]], writes=[wt["Rm"]])
                cur, nxt = nxt, cur
            b_d2 = nb()
            for hh in R4:
                o_ = pf(b_d2, hh)
                sc.op("pe", lambda e, hh=hh, o_=o_: e.matmul(o_, Gm[:, hh, :], self.tri_bf, start=True, stop=False),
                      reads=[wt["Gm"], self.t_const], writes=[pt_(b_d2)])
                sc.op("pe", lambda e, hh=hh, o_=o_: e.matmul(o_, GmL[:, hh, :], self.tri_bf, start=False, stop=False),
                      reads=[wt["GmL"], self.t_const], writes=[pt_(b_d2)])
                sc.op("pe", lambda e, hh=hh, o_=o_: e.matmul(o_, self.ident, cb1(4, hh).to_broadcast([128, 128]), start=False, stop=False),
                      reads=[tsb, self.t_const], writes=[pt_(b_d2)])
                sc.op("pe", lambda e, hh=hh, o_=o_: e.matmul(o_, self.ident, cb1(5, hh).to_broadcast([128, 128]), start=False, stop=False),
                      reads=[tsb, self.t_const], writes=[pt_(b_d2)])
                sc.op("pe", lambda e, hh=hh, o_=o_: e.matmul(o_, self.ident, self.C2_bf, start=False, stop=True),
                      reads=[self.t_const], writes=[pt_(b_d2)])
            b_kq = nb()
            for hh in R4:
                sc.op("pe", lambda e, hh=hh: e.matmul(pf(b_kq, hh), kT[:, H[hh], tsl], qT[:, H[hh], tsl], start=True, stop=True),
                      reads=[t_k, t_q], writes=[pt_(b_kq)])
            b_w = nb()
            for hh in R4:
                sc.op("pe", lambda e, hh=hh: e.matmul(pf(b_w, hh), kbg[:, hh, :], Rm[:, hh, :], start=True, stop=True),
                      reads=[wt["kbg"], wt["Rm"]], writes=[pt_(b_w)])
            yield
            sc.op("act", lambda e: e.activation(out=E, in_=f4(b_d2), func=AF.Exp, bias=-BIGC), reads=[pt_(b_d2)], writes=[wt["E"]])
            sc.op("act", lambda e: e.activation(out=nwT, in_=f4(b_w), func=AF.Copy, scale=-1.0), reads=[pt_(b_w)], writes=[wt["nwT"]])
            sc.op("dve", lambda e: e.tensor_tensor(attnT, f4(b_kq), E, ALU.mult), reads=[pt_(b_kq), wt["E"]], writes=[wt["attnT"]])
            yield
            b_v = nb()
            for hh in R4:
                sc.op("pe", lambda e, hh=hh: e.matmul(pf(b_v, hh), Rm[:, hh, :], vb[:, hh, :], start=True, stop=False),
                      reads=[wt["Rm"], wt["vb"]], writes=[pt_(b_v)])
                sc.op("pe", lambda e, hh=hh: e.matmul(pf(b_v, hh), nwT[:, hh, :], self.Sbf[:, H[hh], :], start=False, stop=True),
                      reads=[wt["nwT"], t_Sbf], writes=[pt_(b_v)])
            b_qs = nb()
            for hh in R4:
                sc.op("pe", lambda e, hh=hh: e.matmul(pf(b_qs, hh), qT[:, H[hh], tsl], self.Sbf[:, H[hh], :], start=True, stop=True),
                      reads=[t_q, t_Sbf], writes=[pt_(b_qs)])
            yield
            for hh in R4:
                sc.op("act", lambda e, hh=hh: e.activation(out=vnew[:, hh, :], in_=pf(b_v, hh), func=AF.Copy, scale=c1(SIRK, hh)),
                      reads=[pt_(b_v), ts], writes=[wt["vnew"]])
            yield
            b_av = nb()
            for hh in R4:
                sc.op("pe", lambda e, hh=hh: e.matmul(pf(b_av, hh), attnT[:, hh, :], vnew[:, hh, :], start=True, stop=True),
                      reads=[wt["attnT"], wt["vnew"]], writes=[pt_(b_av)])
            b_s = nb()
            for hh in R4:
                sc.op("pe", lambda e, hh=hh: e.matmul(pf(b_s, hh), kd[:, hh, :], vnew[:, hh, :], start=True, stop=True),
                      reads=[wt["kd"], wt["vnew"]], writes=[pt_(b_s)])
            yield
            for hh in R4:
                sc.op("dve", lambda e, hh=hh: e.scalar_tensor_tensor(self.Sst[:, H[hh], :], self.Sst[:, H[hh], :], c1(SCD, hh), pf(b_s, hh), ALU.mult, ALU.add),
                      reads=[pt_(b_s), ts, t_S], writes=[t_S])
            sc.op("act", lambda e: e.activation(out=self.Sbf[:, hg * NHT:(hg + 1) * NHT, :], in_=self.Sst[:, hg * NHT:(hg + 1) * NHT, :], func=AF.Copy),
                  reads=[t_S], writes=[t_Sbf])
            sc.op("act", lambda e: e.activation(out=avsb, in_=f4(b_av), func=AF.Copy), reads=[pt_(b_av)], writes=[wt["avsb"]])
            for hh in R4:
                sc.op("dve", lambda e, hh=hh: e.scalar_tensor_tensor(opp[:, hh, :], pf(b_qs, hh), c1(SEG, hh), avsb[:, hh, :], ALU.mult, ALU.add),
                      reads=[pt_(b_qs), ts, wt["avsb"]], writes=[wt["opp"]])
            yield
            sc.op("act", lambda e: e.activation(out=avsb, in_=opp, func=AF.Square), reads=[wt["opp"]], writes=[wt["avsb"]])
            sc.op("dve", lambda e: e.tensor_reduce(osc[:, 0:NHT], avsb, mybir.AxisListType.X, ALU.add), reads=[wt["avsb"]], writes=[wt["osc"]])
            c0 = col(0)
            sqv = scal[:, SSQ, c0:c0 + NHT]
            sc.op("dve", lambda e: e.tensor_tensor(osc[:, NHT:2 * NHT], sqv, sqv, ALU.mult), reads=[ts, wt["osc"]], writes=[wt["osc"]])
            sc.op("dve", lambda e: e.tensor_tensor(osc[:, NHT:2 * NHT], osc[:, NHT:2 * NHT], osc[:, 0:NHT], ALU.mult), reads=[wt["osc"]], writes=[wt["osc"]])
            sc.op("act", lambda e: e.activation(out=osc[:, NHT:2 * NHT], in_=osc[:, NHT:2 * NHT], func=AF.Ln, scale=1.0 / 128, bias=EPS), reads=[wt["osc"]], writes=[wt["osc"]])
            sc.op("act", lambda e: e.activation(out=osc[:, NHT:2 * NHT], in_=osc[:, NHT:2 * NHT], func=AF.Exp, scale=-0.5), reads=[wt["osc"]], writes=[wt["osc"]])
            sc.op("dve", lambda e: e.tensor_tensor(osc[:, NHT:2 * NHT], osc[:, NHT:2 * NHT], sqv, ALU.mult), reads=[ts, wt["osc"]], writes=[wt["osc"]])
            yield
            for hh in R4:
                sc.op("act", lambda e, hh=hh: e.activation(out=onb[:, hh, :], in_=opp[:, hh, :], func=AF.Copy, scale=osc[:, NHT + hh:NHT + hh + 1]),
                      reads=[wt["opp"], wt["osc"]], writes=[wt["onb"]])
            b_t = nb()
            for hh in R4:
                sc.op("pe", lambda e, hh=hh: e.transpose(pb(b_t, hh), onb[:, hh, :], self.ident),
                      reads=[wt["onb"], self.t_const], writes=[pt_(b_t)])
            yield
            sc.op("dve", lambda e: e.scalar_tensor_tensor(og[:, hg * NHT:(hg + 1) * NHT, tsl], pb(b_t).rearrange("p (a b) -> p a b", a=NHT),
                                                          self.onw[:, 0:1], zs[:, hg * NHT:(hg + 1) * NHT, tsl], ALU.mult, ALU.mult),
                  reads=[pt_(b_t), self.t_lconst, t_z], writes=[t_og])

        prog = [0] * NTH

        def hthread(hg):
            for cb in range(8):
                yield from body(cb, hg)
                prog[hg] = cb + 1

        def evac_o(nch, h, ps, pt):
            sl = slice(h * HALF, (h + 1) * HALF)
            sc.op("dve", lambda e: e.tensor_tensor(self.xT[:, nch, sl], self.xT[:, nch, sl], ps, ALU.add),
                  reads=[pt, self.t_xk[nch]], writes=[self.t_xk[nch]])
        t_up_h = [Tok("up_h0"), Tok("up_h1")]

        def post_g(h, lin_banks, aux_banks):
            hs = (h,) if isinstance(h, int) else tuple(h)
            tu = t_up_h[hs[0]]
            self.merge_deps(tu, [t_q, t_k, t_v, t_z, t_sq])
            yield from self.linear_fm_g(self.a_w_out[ai], D, 0, D, og, t_ogs, evac_o, banks=lin_banks, halves=hs)
            yield from self.mlp_ple_g(li, ti, halves=hs, lin_banks=lin_banks, aux_banks=aux_banks, t_up=tu, prefetch=False)

        ths = [hthread(i_) for i_ in range(NTH)]
        alive = [True] * NTH
        step = 0
        DELAY = getattr(self, "gdn_delay", 3)
        post0 = None
        post0_alive = False
        while any(alive):
            for i_, th in enumerate(ths):
                if not alive[i_]:
                    continue
                if step < DELAY * i_:
                    continue
                try:
                    next(th)
                except StopIteration:
                    alive[i_] = False
            if OVERLAP and post0 is None and min(prog) >= 4:
                post0 = post_g(0, (6, 7), (6,))
                post0_alive = True
            if post0_alive:
                try:
                    next(post0)
                except StopIteration:
                    post0_alive = False
            step += 1
        if OVERLAP:
            if post0 is None:
                post0 = post_g(0, (6, 7), (6,))
            for _ in post0:
                pass
            for _ in post_g(1, (0, 1), (2, 3)):
                pass
        else:
            for _ in post_g((0, 1), (0, 1), (2, 3)):
                pass
        allt = [t_q, t_k, t_v, t_z, t_sq] + t_pc + t_diag + [wts[g_][n_] for g_ in range(len(wts)) for n_ in names] + t_ogs + t_up_h
        self.merge_deps(self.t_big0, allt)
        self.merge_deps(self.t_big1, allt)

    def attn_consts_np(self):
        S = self.S
        inv = np.power(np.float32(500000.0), -np.arange(0, 32, 2, dtype=np.float32) / np.float32(32)).astype(np.float32)
        ang = (np.arange(S, dtype=np.float32)[:, None] * inv[None, :]).astype(np.float32)
        cos = np.cos(ang).astype(np.float32)
        sin = np.sin(ang).astype(np.float32)
        C32 = np.concatenate([cos.T, cos.T], axis=0)
        S32 = np.concatenate([sin.T, sin.T], axis=0)
        prot = np.zeros((128, 128), np.float32)
        for m_ in range(16):
            prot[m_ + 16, m_] = -1.0
            prot[m_, m_ + 16] = 1.0
        kj = np.arange(128)[:, None]

        def mk(delta, nq):
            qi = np.arange(nq)[None, :]
            dist = qi + delta - kj
            return np.where((dist >= 0) & (dist <= 128), 0.0, NEG).astype(np.float32)
        masks = {}
        masks[(128, 128)] = mk(128, 128)
        masks[(0, 256)] = mk(0, 256)
        for dl in (0, 64, 128, 192):
            masks[(dl, 64)] = mk(dl, 64)
        return C32, S32, prot, masks

    def setup_attn_consts(self):
        sc = self.sc
        C32, S32, prot, masks = self.attn_consts_np()
        self.d_cos = self.add_const("c_cos", C32)
        self.d_sin = self.add_const("c_sin", S32)
        d_prot = self.add_const("c_prot", prot.astype(ml_dtypes.bfloat16))
        cw = [self.t_const, self.t_S, self.t_Sbf] + self.t_Sg + self.t_Sbfg
        sc.dma("sp", "cst", lambda e: e.dma_start(out=self.protT, in_=d_prot), writes=cw)
        self.mask_ap = {}
        off = 0
        for key, arr in masks.items():
            d_ = self.add_const("c_mask_%d_%d" % key, arr.astype(ml_dtypes.bfloat16))
            dst = self.maskbuf[:, off:off + key[1]]
            off += key[1]
            self.mask_ap[key] = dst
            sc.dma("sp", "cst", lambda e, d_=d_, dst=dst: e.dma_start(out=dst, in_=d_), writes=cw)

    def kv_phase(self, src, src_toks):
        sc = self.sc
        S = self.S
        ST = min(2048, S)
        L = self.n_layers
        GD = (1, 4, 16)
        hT2 = self.big(BF16, [KC, ST], 0)
        t_h2 = Tok("h2")
        sqs = self.big(BF16, [KC, T], 32768)
        t_sqs = Tok("sqs")
        cs = self.big(F32, [2, ST], 49152)
        t_cs = Tok("cs")
        knat = [self.big(BF16, [HALF], 65536 + i * 1024) for i in range(2)]
        t_knat = [Tok("knat0"), Tok("knat1")]
        kperm = [self.big(BF16, [ST], 67584 + i * 4096) for i in range(2)]
        t_kperm = [Tok("kperm0"), Tok("kperm1")]
        rt = self.tmpf
        t_rt = self.t_tmpf
        vtmp = [self.big(BF16, [256], 79872 + i * 512) for i in range(4)]
        t_vtmp = [Tok("vt%d" % i) for i in range(4)]
        allt = [t_h2, t_sqs, t_cs] + t_knat + t_kperm + t_vtmp
        for t_ in allt:
            self.merge_deps(t_, [self.t_big0, self.t_big1])
        wkv = self.b_w_kv
        for st in range(S // ST):
            for half in range(ST // T):
                ti = st * (ST // T) + half
                self.load_x(src, src_toks[ti] if src_toks is not None else None, ti)
                self.rmsnorm(2 * L + 1, hT2[:, :, half * T:(half + 1) * T], t_h2, sqs, t_sqs)
            sc.dma("sp", "csld", lambda e, st=st: e.dma_start(out=cs[0:32, 0, :], in_=self.d_cos[:, st * ST:(st + 1) * ST]), writes=[t_cs])
            sc.dma("sp", "csld", lambda e, st=st: e.dma_start(out=cs[0:32, 1, :], in_=self.d_sin[:, st * ST:(st + 1) * ST]), writes=[t_cs])
            nh = ST // HALF

            kdef = [None]
            kctr = [0]

            def k_tail(nch, h, par, st=st):
                g = nch // 2
                d = GD[g]
                kn, tkn = knat[par], t_knat[par]
                kp, tkp = kperm[nch % 2], t_kperm[nch % 2]
                r_, tr_ = rt[par], t_rt[par]
                sl = slice(h * HALF, (h + 1) * HALF)
                b = 2 + par
                sc.op("pe", lambda e: e.matmul(self.psf(b), self.protT, kn, start=True, stop=True),
                      reads=[tkn, self.t_const], writes=[self.ps_tok[b]])
                sc.op("dve", lambda e: e.tensor_tensor(r_[0:32, :], self.psf(b)[0:32, :], cs[0:32, 1, sl], ALU.mult),
                      reads=[self.ps_tok[b], t_cs], writes=[tr_])
                sc.op("dve", lambda e: e.tensor_tensor(kn[0:32, :], kn[0:32, :], cs[0:32, 0, sl], ALU.mult),
                      reads=[t_cs, tkn], writes=[tkn])
                sc.op("dve", lambda e: e.tensor_tensor(kn[0:32, :], kn[0:32, :], r_[0:32, :], ALU.add),
                      reads=[tr_, tkn], writes=[tkn])
                jn = HALF // d
                dst = kp.rearrange("p (r j) -> p r j", r=d)[:, :, h * jn:(h + 1) * jn]
                srcv = kn.rearrange("p (j r) -> p r j", r=d)
                sc.op("act", lambda e: e.activation(out=dst, in_=srcv, func=AF.Copy), reads=[tkn], writes=[tkp])
                if h == nh - 1:
                    jt = ST // d
                    dd = self.K_dram[nch].rearrange("p (r j) -> p r j", r=d)[:, :, st * jt:(st + 1) * jt]
                    sc.dma("sp", "kst%d" % (nch % 2), lambda e: e.dma_start(out=dd, in_=kp.rearrange("p (r j) -> p r j", r=d)),
                           reads=[tkp], writes=[self.t_kdram2[nch % 2]])

            def evac_k(nch, h, ps, pt):
                par = kctr[0] % 2
                kctr[0] += 1
                kn, tkn = knat[par], t_knat[par]
                sc.op("act", lambda e: e.activation(out=kn, in_=ps, func=AF.Copy), reads=[pt], writes=[tkn])
                if kdef[0] is not None:
                    kdef[0]()
                kdef[0] = lambda: k_tail(nch, h, par)
            self.linear_fm(wkv, D, 0, 768, hT2, [t_h2], evac_k, ntok=ST, banks=(0, 1))
            if kdef[0] is not None:
                kdef[0]()
                kdef[0] = None
            wv = wkv.rearrange("(kc p) n -> p kc n", p=128)
            vi = 0
            for g in range(3):
                d = GD[g]
                slab, stok = self.load_slab(wv[:, :, 768 + g * 256:768 + (g + 1) * 256], KC, 256)
                for r in range(d):
                    for jb in range(ST // (128 * d)):
                        b = vi % 4
                        par = vi % 4
                        vi += 1
                        t_lo = r + d * 128 * jb
                        cols = slice(t_lo, t_lo + d * 127 + 1, d)
                        for k in range(KC):
                            sc.op("pe", lambda e, b=b, k=k, cols=cols, slab=slab: e.matmul(
                                self.psf(b)[:, 0:256], hT2[:, k, cols], slab[:, k, :], start=(k == 0), stop=(k == KC - 1)),
                                reads=[t_h2, stok], writes=[self.ps_tok[b]])
                        if par % 2 == 0:
                            sc.op("act", lambda e, b=b, par=par: e.activation(out=vtmp[par], in_=self.psf(b)[:, 0:256], func=AF.Copy),
                                  reads=[self.ps_tok[b]], writes=[t_vtmp[par]])
                        else:
                            sc.op("dve", lambda e, b=b, par=par: e.tensor_copy(vtmp[par], self.psf(b)[:, 0:256]),
                                  reads=[self.ps_tok[b]], writes=[t_vtmp[par]])
                        row0 = r * (S // d) + st * (ST // d) + 128 * jb
                        sc.dma("sp", "vst%d" % par, lambda e, g=g, row0=row0, par=par: e.dma_start(
                            out=self.V_dram[g, row0:row0 + 128, :], in_=vtmp[par]), reads=[t_vtmp[par]], writes=[self.t_vdram2[par]])
        self.merge_deps(self.t_big0, allt)
        self.merge_deps(self.t_big1, allt)

    def attn_tile(self, li, ti):
        sc = self.sc
        S = self.S
        bj = li - self.n_a
        GD = (1, 4, 16)
        t0 = ti * T
        qperm = self.big(BF16, [12, T], 0)
        t_qp = Tok("qp")
        sqs = self.big(BF16, [KC, T], 0)
        off = 24576
        Kw, Vw, kwin_meta = [], [], []
        for g in range(3):
            d = GD[g]
            jlo = max(0, ti * T // d - 128) // 128
            jhi = ((ti + 1) * T // d - 1) // 128
            nb = jhi - jlo + 1
            kw = self.big(BF16, [2, d, nb * 128], off)
            off += 2 * d * nb * 128 * 2
            vw = self.big(BF16, [d, nb, 256], off)
            off += d * nb * 256 * 2
            Kw.append(kw)
            Vw.append(vw)
            kwin_meta.append((jlo, nb))
        cs = self.big(F32, [2, T], off)
        off += 8192
        qnat = [self.big(BF16, [HALF], off + i * 1024) for i in range(2)]
        off += 2048
        rt = self.tmpf
        PT = [self.big(BF16, [HALF], off + i * 1024) for i in range(2)]
        off += 2048
        acc_d = self.rstd
        ao = self.big(BF16, [4, T], off)
        off += 8192
        acc_o = self.big(F32, [T], off)
        off += 4096
        assert off <= self.big_n, off
        t_kw = [Tok("kw%d" % g) for g in range(3)]
        t_vw = [Tok("vw%d" % g) for g in range(3)]
        t_cs, t_rden = Tok("cs"), self.t_rstd
        t_qnat = [Tok("qn0"), Tok("qn1")]
        t_rt = self.t_tmpf
        t_PT = [Tok("PT0"), Tok("PT1")]
        t_ao = Tok("ao")
        t_acc = Tok("acc")
        allt = [t_qp, t_cs, t_ao, t_acc] + t_kw + t_vw + t_qnat + t_PT
        for t_ in allt:
            self.merge_deps(t_, [self.t_big0, self.t_big1])
        self.rmsnorm(li, self.hT, self.t_h, sqs, t_qp)
        sc.dma("sp", "csld", lambda e: e.dma_start(out=cs[0:32, 0, :], in_=self.d_cos[:, t0:t0 + T]), writes=[t_cs])
        sc.dma("sp", "csld", lambda e: e.dma_start(out=cs[0:32, 1, :], in_=self.d_sin[:, t0:t0 + T]), writes=[t_cs])
        for g in range(3):
            d = GD[g]
            jlo, nb = kwin_meta[g]
            for kvh in range(2):
                srck = self.K_dram[2 * g + kvh].rearrange("p (r j) -> p r j", r=d)[:, :, jlo * 128:(jlo + nb) * 128]
                sc.dma("sp", "kwl%d" % g, lambda e, g=g, kvh=kvh, srck=srck: e.dma_start(out=Kw[g][:, kvh, :, :], in_=srck),
                       reads=self.t_kdram2, writes=[t_kw[g]])
            srcv = self.V_dram[g].rearrange("(r jb kj) f -> kj r jb f", r=d, kj=128)
            if d == 1:
                sc.dma("sp", "vwl%d" % g, lambda e, g=g, srcv=srcv, jlo=jlo, nb=nb: e.dma_start(
                    out=Vw[g][:, 0, :, :], in_=srcv[:, 0, jlo:jlo + nb, :]), reads=self.t_vdram2, writes=[t_vw[g]])
            else:
                for jbi in range(nb):
                    sc.dma("sp", "vwl%d" % g, lambda e, g=g, srcv=srcv, jlo=jlo, jbi=jbi: e.dma_start(
                        out=Vw[g][:, :, jbi, :], in_=srcv[:, :, jlo + jbi, :]), reads=self.t_vdram2, writes=[t_vw[g]])
        wq = self.b_w_q[bj]
        scale = float(128 ** -0.5)
        qctr = [0]
        deferred = [None]
        for hh in range(2):
            for g in range(3):
                d = GD[g]

                def rope_tail(nch, h, par, g=g, d=d):
                    qn, tqn = qnat[par], t_qnat[par]
                    r_, tr_ = rt[par], t_rt[par]
                    sl = slice(h * HALF, (h + 1) * HALF)
                    b = 2 + par
                    sc.op("pe", lambda e: e.matmul(self.psf(b), self.protT, qn, start=True, stop=True),
                          reads=[tqn, self.t_const], writes=[self.ps_tok[b]])
                    sc.op("dve", lambda e: e.tensor_tensor(r_[0:32, :], self.psf(b)[0:32, :], cs[0:32, 1, sl], ALU.mult),
                          reads=[self.ps_tok[b], t_cs], writes=[tr_])
                    sc.op("dve", lambda e: e.tensor_tensor(qn[0:32, :], qn[0:32, :], cs[0:32, 0, sl], ALU.mult),
                          reads=[t_cs, tqn], writes=[tqn])
                    sc.op("dve", lambda e: e.tensor_tensor(qn[0:32, :], qn[0:32, :], r_[0:32, :], ALU.add),
                          reads=[tr_, tqn], writes=[tqn])
                    jn = HALF // d
                    dst = qperm[:, g * 4 + nch, :].rearrange("p (r j) -> p r j", r=d)[:, :, h * jn:(h + 1) * jn]
                    srcv = qn.rearrange("p (j r) -> p r j", r=d)
                    sc.op("act", lambda e: e.activation(out=dst, in_=srcv, func=AF.Copy), reads=[tqn], writes=[t_qp])

                def evac_q(nch, h, ps, pt, g=g, d=d, rope_tail=rope_tail):
                    par = qctr[0] % 2
                    qctr[0] += 1
                    qn, tqn = qnat[par], t_qnat[par]
                    sc.op("act", lambda e: e.activation(out=qn, in_=ps, func=AF.Copy), reads=[pt], writes=[tqn])
                    if deferred[0] is not None:
                        deferred[0]()
                    deferred[0] = lambda: rope_tail(nch, h, par)
                c0 = g * 1024 + hh * 512
                self.linear_fm(wq, D, c0, c0 + 512, self.hT, [self.t_h], evac_q)
            if deferred[0] is not None:
                deferred[0]()
                deferred[0] = None
            for hq4 in range(4):
                hq = hh * 4 + hq4
                kvh = hq // 4
                for g in range(3):
                    d = GD[g]
                    jlo, nb = kwin_meta[g]
                    nqb = T // d
                    i_lo = ti * nqb
                    i_hi = i_lo + nqb
                    qv = qperm[:, g * 4 + hq4, :].rearrange("p (r j) -> p r j", r=d)
                    st_ = {"col": 0, "items": [], "stash": None, "first": {4: True, 5: True, 6: True, 7: True}, "sb": 0}

                    def emit_pv(stash, g=g, st_=st_):
                        if stash is None:
                            return
                        pt_ap, tpt, items = stash
                        for (c_lo, nq, vblk, col0) in items:
                            q0 = 0
                            while q0 < nq:
                                c_ = col0 + q0
                                bank = 4 + c_ // HALF
                                n_ = min(nq - q0, HALF - c_ % HALF)
                                rhs = pt_ap[:, c_lo + q0:c_lo + q0 + n_]
                                osl = slice(c_ % HALF, c_ % HALF + n_)
                                f1 = st_["first"][bank]
                                st_["first"][bank] = False
                                sc.op("pe", lambda e, rhs=rhs, bank=bank, osl=osl, vblk=vblk, f1=f1: e.matmul(
                                    self.psf(bank)[:, osl], vblk, rhs, start=f1, stop=True, skip_group_check=True),
                                    reads=[tpt, t_vw[g]], writes=[self.ps_tok[bank]])
                                f2 = st_["first"][bank + 2]
                                st_["first"][bank + 2] = False
                                sc.op("pe", lambda e, rhs=rhs, bank=bank, osl=osl, f2=f2: e.matmul(
                                    self.psf(bank + 2)[:, osl], self.ones, rhs, start=f2, stop=True, skip_group_check=True),
                                    reads=[tpt, self.t_const], writes=[self.ps_tok[bank + 2]])
                                q0 += n_

                    def flush(final=False, st_=st_, emit_pv=emit_pv):
                        if st_["items"]:
                            b = 2 + st_["sb"] % 2
                            par = st_["sb"] % 2
                            st_["sb"] += 1
                            ncol = st_["col"]
                            pt_ap, tpt = PT[par], t_PT[par]
                            sc.op("act", lambda e: e.activation(out=pt_ap[:, 0:ncol], in_=self.psf(b)[:, 0:ncol], func=AF.Exp, scale=scale),
                                  reads=[self.ps_tok[b]], writes=[tpt])
                            emit_pv(st_["stash"])
                            st_["stash"] = (pt_ap, tpt, st_["items"])
                            st_["col"] = 0
                            st_["items"] = []
                            st_["masks"] = []
                        if final:
                            emit_pv(st_["stash"])
                            st_["stash"] = None
                    st_["masks"] = []
                    for r in range(d):
                        for jb in range(jlo, jlo + nb):
                            qs = max(i_lo, 128 * jb)
                            qe = min(i_hi, 128 * jb + 256)
                            if qe <= qs:
                                continue
                            nq = qe - qs
                            delta = qs - 128 * jb
                            if nq == 64:
                                mk_ = self.mask_ap[(delta, 64)]
                            elif delta == 128:
                                mk_ = self.mask_ap[(128, 128)]
                            else:
                                mk_ = self.mask_ap[(0, 256)][:, 0:nq]
                            if st_["col"] + nq > HALF:
                                flush()
                            b = 2 + st_["sb"] % 2
                            c_lo = st_["col"]
                            kblk = Kw[g][:, kvh, r, (jb - jlo) * 128:(jb - jlo + 1) * 128]
                            vblk = Vw[g][:, r, jb - jlo, kvh * 128:(kvh + 1) * 128]
                            qblk = qv[:, r, qs - i_lo:qe - i_lo]
                            out = self.psf(b)[:, c_lo:c_lo + nq]
                            sc.op("pe", lambda e, out=out, kblk=kblk, qblk=qblk: e.matmul(out, kblk, qblk, start=True, stop=False),
                                  reads=[t_kw[g], t_qp], writes=[self.ps_tok[b]])
                            sc.op("pe", lambda e, out=out, mk_=mk_: e.matmul(out, self.ident, mk_, start=False, stop=True),
                                  reads=[self.t_const], writes=[self.ps_tok[b]])
                            st_["items"].append((c_lo, nq, vblk, r * nqb + (qs - i_lo)))
                            st_["masks"].append((c_lo, nq, mk_))
                            st_["col"] += nq
                    flush(final=True)
                    for half in range(2):
                        sl = slice(half * HALF, (half + 1) * HALF)
                        if d == 1:
                            sc.op("act", lambda e, half=half, sl=sl: e.activation(out=acc_o[:, sl], in_=self.psf(4 + half), func=AF.Copy),
                                  reads=[self.ps_tok[4 + half]], writes=[t_acc])
                            sc.op("dve", lambda e, half=half, sl=sl: e.tensor_copy(acc_d[:, sl], self.psf(6 + half)),
                                  reads=[self.ps_tok[6 + half]], writes=[t_rden])
                        else:
                            rpb = HALF // nqb
                            for (acc, tk, bk) in ((acc_o, t_acc, 4 + half), (acc_d, t_rden, 6 + half)):
                                av_ = acc.rearrange("p (j r) -> p r j", r=d)[:, half * rpb:(half + 1) * rpb, :]
                                pv_ = self.psf(bk).rearrange("p (r j) -> p r j", r=rpb)
                                sc.op("dve", lambda e, av_=av_, pv_=pv_: e.tensor_tensor(av_, av_, pv_, ALU.add),
                                      reads=[self.ps_tok[bk], tk], writes=[tk])
                sc.op("act", lambda e: e.activation(out=acc_d, in_=acc_d, func=AF.Ln), reads=[t_rden], writes=[t_rden])
                sc.op("act", lambda e: e.activation(out=acc_d, in_=acc_d, func=AF.Exp, scale=-1.0), reads=[t_rden], writes=[t_rden])
                for half in range(2):
                    sl = slice(half * HALF, (half + 1) * HALF)
                    sc.op("dve", lambda e, sl=sl, hq4=hq4: e.tensor_tensor(ao[:, hq4, sl], acc_o[:, sl], acc_d[:, sl], ALU.mult),
                          reads=[t_acc, t_rden], writes=[t_ao])

            def evac_o(nch, h, ps, pt):
                sl = slice(h * HALF, (h + 1) * HALF)
                sc.op("dve", lambda e: e.tensor_tensor(self.xT[:, nch, sl], self.xT[:, nch, sl], ps, ALU.add),
                      reads=[pt, self.t_xk[nch]], writes=[self.t_xk[nch]])
            self.linear_fm(self.b_w_o[bj][hh * 512:(hh + 1) * 512, :], 512, 0, D, ao, [t_ao], evac_o)

        self.merge_deps(self.t_big0, allt)
        self.merge_deps(self.t_big1, allt)

    def load_x(self, src, src_tok, ti):
        t0 = ti * T
        sv_ = src.rearrange("(kc p) s -> p kc s", p=128)
        for k in range(KC):
            self.sc.dma("sp", "xld%d" % k, lambda e, k=k: e.dma_start(out=self.xT[:, k, :], in_=sv_[:, k, t0:t0 + T]),
                        reads=[src_tok[k]] if src_tok is not None else [], writes=[self.t_xk[k]])

    def store_x(self, ti):
        t0 = ti * T
        dv_ = self.x_scr.rearrange("(kc p) s -> p kc s", p=128)
        for k in range(KC):
            self.sc.dma("sp", "xst%d" % k, lambda e, k=k: e.dma_start(out=dv_[:, k, t0:t0 + T], in_=self.xT[:, k, :]),
                        reads=[self.t_xk[k]], writes=[self.t_xscr[ti][k]])

    def final_out(self, ti):
        t0 = ti * T
        L = self.n_layers
        sq = self.big(BF16, [KC, T], 0)
        of = self.big(F32, [KC, T], 65536)
        self.rmsnorm(2 * L, of, self.t_big1, sq, self.t_big0)
        self.sc.dma("sp", "ost", lambda e: e.dma_start(
            out=self.outT.rearrange("(kc p) s -> p kc s", p=128)[:, :, t0:t0 + T], in_=of),
            reads=[self.t_big1], writes=[self.t_out])

    def build(self):
        self.declare()
        self.alloc()
        self.t_big0 = Tok("big0")
        self.t_big1 = Tok("big1")
        self.t_pT = Tok("pT")
        self.setup_consts()
        L = self.n_layers
        for li in range(L):
            if self.mixers and li == self.n_a:
                self.setup_attn_consts()
                if li == 0:
                    self.kv_phase(self.xT_in, None)
                else:
                    self.kv_phase(self.x_scr, self.t_xscr)
            for ti in range(self.NT):
                if li == 0:
                    self.load_x(self.xT_in, None, ti)
                else:
                    self.load_x(self.x_scr, self.t_xscr[ti], ti)
                if self.mixers and li < self.n_a:
                    if ti == 0:
                        self.gdn_layer_consts(li)
                    self.gdn_tile(li, ti)
                elif self.mixers:
                    self.attn_tile(li, ti)
                    self.mlp_ple(li, ti)
                else:
                    self.mlp_ple(li, ti)
                if li == L - 1:
                    self.final_out(ti)
                else:
                    self.store_x(ti)
        self.sc.wait_all("sp", [self.t_out])
        self.sc.emit()
        return self.nc


_CACHE = {}


def host_inputs(S, b, inputs, gen):
    m = {}
    m["xT"] = np.ascontiguousarray(inputs["x"][b].T)
    L = gen.n_layers
    m["pT"] = np.ascontiguousarray(np.transpose(inputs["p"][:L, b], (0, 2, 1)))
    m["final_norm"] = np.ascontiguousarray(inputs["final_norm"])
    for k in ["attn_norm", "mlp_norm", "mlp_w_up", "mlp_w_down", "ple_w_proj", "ple_w_gate"]:
        m[k] = np.ascontiguousarray(inputs[k][:L])
    na = max(1, gen.n_a)
    m["a_w_in"] = np.ascontiguousarray(inputs["a_w_in"][:na])
    m["a_w_out"] = np.ascontiguousarray(inputs["a_w_out"][:na])
    cw = inputs["a_conv_w"][:na]
    m["convw"] = np.ascontiguousarray(cw.reshape(na, 4, 24, 128).transpose(0, 3, 1, 2))
    m["alog_rep"] = np.ascontiguousarray(np.broadcast_to(np.tile(inputs["a_log"][:na], (1, 8))[:, None, :], (na, 128, 64)))
    m["dtb_rep"] = np.ascontiguousarray(np.broadcast_to(np.tile(inputs["a_dt_bias"][:na], (1, 8))[:, None, :], (na, 128, 64)))
    m["onw"] = np.ascontiguousarray(inputs["a_out_norm"][:na][:, :, None])
    nbl = max(1, L - gen.n_a)
    m["b_w_kv"] = np.ascontiguousarray(inputs["b_w_kv"])
    m["b_w_q"] = np.ascontiguousarray(inputs["b_w_q"][:nbl])
    m["b_w_o"] = np.ascontiguousarray(inputs["b_w_o"][:nbl])
    m["kv_norm"] = np.ascontiguousarray(inputs["kv_norm"])
    for k, v in gen.consts_np.items():
        m[k] = v
    return m


def kernel(**inputs):
    inputs = {k: np.asarray(v) for k, v in inputs.items()}
    B, S, _ = inputs["x"].shape
    key = (S,)
    if key not in _CACHE:
        g = Gen(S)
        g.build()
        _CACHE[key] = g
    g = _CACHE[key]
    in_maps = [host_inputs(S, b, inputs, g) for b in range(B)]
    res = run_bass_kernel_spmd(g.nc, in_maps, core_ids=list(range(B)))
    out = np.stack([np.ascontiguousarray(r["outT"].T) for r in res.results], axis=0)
    return out.astype(np.float32)
```
